# Optimizing a Trainium2 kernel written in Bass

```python
import jax, jax.numpy as jnp
from jax import lax
import numpy as np

D_MODEL = 2048
BATCH = 8
SEQ = 2048
DEPTH = 4

N_MIXERS = 2
N_META = 16
NORM_EPS = 1e-6
L2_EPS = 1e-6
D_FF = 5504
FFN_RES = 0.5
DN_QK_HEADS = 16
DN_V_HEADS = 32
DN_HEAD_K = 128
DN_HEAD_V = 128
DN_CONV = 4
DN_CHUNK = 64
DN_KEY_DIM = DN_QK_HEADS * DN_HEAD_K
DN_VAL_DIM = DN_V_HEADS * DN_HEAD_V
DN_CONV_DIM = 2 * DN_KEY_DIM + DN_VAL_DIM
DN_PROJ = DN_CONV_DIM + DN_VAL_DIM + 2 * DN_V_HEADS
SWA_Q_HEADS = 32
SWA_KV_HEADS = 4
SWA_HEAD_DIM = 64
SWA_GROUP = SWA_Q_HEADS // SWA_KV_HEADS
SWA_WINDOW = 128
SWA_PROJ = (SWA_Q_HEADS + 2 * SWA_KV_HEADS) * SWA_HEAD_DIM
ROPE_THETA = 10000.0

kernel_name = "hybrid_gdn_swa_sink_macaron_meta"


def rms_norm(x, gain):
    xf = x.astype(jnp.float32)
    y = xf * lax.rsqrt(jnp.mean(xf * xf, axis=-1, keepdims=True) + NORM_EPS)
    return (y * gain.astype(jnp.float32)).astype(x.dtype)


def l2norm(x):
    xf = x.astype(jnp.float32)
    return xf * lax.rsqrt(jnp.sum(xf * xf, axis=-1, keepdims=True) + L2_EPS)


def swiglu(x, w_gu, w_down):
    g, u = jnp.split(x @ w_gu, 2, axis=-1)
    return (jax.nn.silu(g) * u) @ w_down


def rope(x, pos):
    hd = x.shape[-1]
    inv = ROPE_THETA ** (-jnp.arange(0, hd, 2, dtype=jnp.float32) / hd)
    ang = pos.astype(jnp.float32)[:, None] * inv[None, :]
    cos = jnp.cos(ang)[None, :, None, :]
    sin = jnp.sin(ang)[None, :, None, :]
    x1, x2 = jnp.split(x.astype(jnp.float32), 2, axis=-1)
    return jnp.concatenate([x1 * cos - x2 * sin, x2 * cos + x1 * sin], axis=-1).astype(x.dtype)


def causal_depthwise_conv(x, w):
    K, C = w.shape
    return lax.conv_general_dilated(
        x, w[:, None, :].astype(x.dtype), window_strides=(1,), padding=[(K - 1, 0)],
        dimension_numbers=('NWC', 'WIO', 'NWC'), feature_group_count=C)


def gated_deltanet(h, w_in, conv_w, a_log, dt_bias, o_gain, w_out):
    B, L, _ = h.shape
    C = DN_CHUNK
    proj = h @ w_in
    qkv, z, b, a = jnp.split(proj, [DN_CONV_DIM, DN_CONV_DIM + DN_VAL_DIM,
                                    DN_CONV_DIM + DN_VAL_DIM + DN_V_HEADS], axis=-1)
    qkv = jax.nn.silu(causal_depthwise_conv(qkv, conv_w))
    q, k, v = jnp.split(qkv, [DN_KEY_DIM, 2 * DN_KEY_DIM], axis=-1)
    q = l2norm(q.reshape(B, L, DN_QK_HEADS, DN_HEAD_K)) * (DN_HEAD_K ** -0.5)
    k = l2norm(k.reshape(B, L, DN_QK_HEADS, DN_HEAD_K))
    v = v.reshape(B, L, DN_V_HEADS, DN_HEAD_V).astype(jnp.float32)
    beta = jax.nn.sigmoid(b.astype(jnp.float32))
    g = -jnp.exp(a_log.astype(jnp.float32)) * jax.nn.softplus(
        a.astype(jnp.float32) + dt_bias.astype(jnp.float32))

    pad = (-L) % C

    def to_chunks(t):
        t = jnp.pad(t, [(0, 0), (pad, 0)] + [(0, 0)] * (t.ndim - 2))
        n = t.shape[1] // C
        t = t.reshape((B, n, C) + t.shape[2:])
        return jnp.moveaxis(t, (1, 2), (0, 3))

    xs = (to_chunks(q), to_chunks(k), to_chunks(v), to_chunks(g), to_chunks(beta))
    tri_incl = jnp.tril(jnp.ones((C, C), dtype=bool))
    tri_strict = jnp.tril(jnp.ones((C, C), dtype=bool), -1)
    eye = jnp.eye(C, dtype=jnp.float32)
    rep = DN_V_HEADS // DN_QK_HEADS

    def chunk_step(S, inp):
        qc, kc, vc, gc, bc = inp
        qc = jnp.repeat(qc, rep, axis=1)
        kc = jnp.repeat(kc, rep, axis=1)
        gcum = jnp.cumsum(gc, axis=-1)
        decay = jnp.exp(jnp.where(tri_incl, gcum[..., :, None] - gcum[..., None, :], -jnp.inf))
        kb = kc * bc[..., None]
        A = jnp.where(tri_strict, jnp.einsum('bhid,bhjd->bhij', kb, kc) * decay, 0.0)
        rhs = jnp.concatenate([vc * bc[..., None], kb * jnp.exp(gcum)[..., None]], axis=-1)
        sol = lax.linalg.triangular_solve(A + eye, rhs, left_side=True, lower=True)
        u, w = jnp.split(sol, [DN_HEAD_V], axis=-1)
        v_new = u - jnp.einsum('bhcd,bhde->bhce', w, S)
        attn = jnp.einsum('bhid,bhjd->bhij', qc, kc) * decay
        out = (jnp.einsum('bhcd,bhde->bhce', qc * jnp.exp(gcum)[..., None], S)
               + jnp.einsum('bhij,bhje->bhie', attn, v_new))
        g_last = gcum[..., -1:]
        S = (S * jnp.exp(g_last)[..., None]
             + jnp.einsum('bhcd,bhce->bhde', kc * jnp.exp(g_last - gcum)[..., None], v_new))
        return S, out

    S0 = jnp.zeros((B, DN_V_HEADS, DN_HEAD_K, DN_HEAD_V), jnp.float32)
    _, outs = lax.scan(chunk_step, S0, xs)
    o = jnp.moveaxis(outs, (0, 3), (1, 2)).reshape(B, -1, DN_V_HEADS, DN_HEAD_V)[:, pad:]
    zf = z.reshape(B, L, DN_V_HEADS, DN_HEAD_V).astype(jnp.float32)
    o = rms_norm(o, o_gain) * jax.nn.silu(zf)
    return o.reshape(B, L, DN_VAL_DIM).astype(h.dtype) @ w_out


def sliding_window_attention(h, w_qkv, sinks, w_out):
    B, L, _ = h.shape
    M, W, hd = N_META, SWA_WINDOW, SWA_HEAD_DIM
    T = L - M
    nb = T // W
    q, k, v = jnp.split(h @ w_qkv, [SWA_Q_HEADS * hd, (SWA_Q_HEADS + SWA_KV_HEADS) * hd], axis=-1)
    pos = jnp.arange(L, dtype=jnp.int32)
    q = rope(q.reshape(B, L, SWA_Q_HEADS, hd), pos).astype(jnp.float32) * (hd ** -0.5)
    q = q.reshape(B, L, SWA_KV_HEADS, SWA_GROUP, hd)
    k = rope(k.reshape(B, L, SWA_KV_HEADS, hd), pos).astype(jnp.float32)
    v = v.reshape(B, L, SWA_KV_HEADS, hd).astype(jnp.float32)
    qm, qr = q[:, :M], q[:, M:]
    km, kr = k[:, :M], k[:, M:]
    vm, vr = v[:, :M], v[:, M:]
    sink = sinks.astype(jnp.float32).reshape(SWA_KV_HEADS, SWA_GROUP)

    def band(xb):
        prev = jnp.pad(xb, [(0, 0), (1, 0), (0, 0), (0, 0), (0, 0)])[:, :-1]
        return jnp.concatenate([prev, xb], axis=2)

    qb = qr.reshape(B, nb, W, SWA_KV_HEADS, SWA_GROUP, hd)
    kband = band(kr.reshape(B, nb, W, SWA_KV_HEADS, hd))
    vband = band(vr.reshape(B, nb, W, SWA_KV_HEADS, hd))
    s_band = jnp.einsum('bnikgd,bnjkd->bkgnij', qb, kband)
    s_meta = jnp.einsum('bnikgd,bmkd->bkgnim', qb, km)
    ii = jnp.arange(W)[:, None]
    jj = jnp.arange(2 * W)[None, :]
    blk = jnp.arange(nb)[:, None, None]
    band_mask = (jj > ii) & (jj <= ii + W) & ((blk > 0) | (jj >= W))
    s_band = jnp.where(band_mask, s_band, -jnp.inf)
    s_sink = jnp.broadcast_to(sink[None, :, :, None, None, None], s_band.shape[:-1] + (1,))
    p = jax.nn.softmax(jnp.concatenate([s_band, s_meta, s_sink], axis=-1), axis=-1)
    o_r = (jnp.einsum('bkgnij,bnjkd->bnikgd', p[..., :2 * W], vband)
           + jnp.einsum('bkgnim,bmkd->bnikgd', p[..., 2 * W:2 * W + M], vm))
    o_r = o_r.reshape(B, T, SWA_Q_HEADS * hd)

    s_mm = jnp.einsum('bikgd,bjkd->bkgij', qm, km)
    s_mm = jnp.where(jnp.tril(jnp.ones((M, M), dtype=bool)), s_mm, -jnp.inf)
    s_msink = jnp.broadcast_to(sink[None, :, :, None, None], s_mm.shape[:-1] + (1,))
    p_m = jax.nn.softmax(jnp.concatenate([s_mm, s_msink], axis=-1), axis=-1)
    o_m = jnp.einsum('bkgij,bjkd->bikgd', p_m[..., :M], vm).reshape(B, M, SWA_Q_HEADS * hd)

    o = jnp.concatenate([o_m, o_r], axis=1).astype(h.dtype)
    return o @ w_out


def setup_inputs(seed: int = 0) -> dict:
    key = jax.random.key(seed)
    ks = jax.random.split(key, 24)
    n_dn = (DEPTH + N_MIXERS - 1) // N_MIXERS
    n_swa = DEPTH // N_MIXERS
    f32 = jnp.float32

    def nrm(k, shape, scale):
        return jax.random.normal(k, shape, f32) * scale

    def gain(k, shape):
        return 1.0 + 0.1 * jax.random.normal(k, shape, f32)

    dt = jnp.exp(jax.random.uniform(ks[13], (n_dn, DN_V_HEADS), f32, np.log(1e-3), np.log(1e-1)))
    return {
        "x": nrm(ks[0], (BATCH, SEQ, D_MODEL), 1.0),
        "meta_tokens": nrm(ks[1], (N_META, D_MODEL), 1.0),
        "ffn_pre_norm": gain(ks[2], (DEPTH, D_MODEL)),
        "ffn_pre_w_gu": nrm(ks[3], (DEPTH, D_MODEL, 2 * D_FF), D_MODEL ** -0.5),
        "ffn_pre_w_down": nrm(ks[4], (DEPTH, D_FF, D_MODEL), D_FF ** -0.5),
        "mix_norm": gain(ks[5], (DEPTH, D_MODEL)),
        "ffn_post_norm": gain(ks[6], (DEPTH, D_MODEL)),
        "ffn_post_w_gu": nrm(ks[7], (DEPTH, D_MODEL, 2 * D_FF), D_MODEL ** -0.5),
        "ffn_post_w_down": nrm(ks[8], (DEPTH, D_FF, D_MODEL), D_FF ** -0.5),
        "dn_w_in": nrm(ks[9], (n_dn, D_MODEL, DN_PROJ), D_MODEL ** -0.5),
        "dn_conv_w": nrm(ks[10], (n_dn, DN_CONV, DN_CONV_DIM), DN_CONV ** -0.5),
        "dn_a_log": jnp.log(jax.random.uniform(ks[11], (n_dn, DN_V_HEADS), f32, 1.0, 16.0)),
        "dn_dt_bias": dt + jnp.log(-jnp.expm1(-dt)),
        "dn_out_norm": gain(ks[12], (n_dn, DN_HEAD_V)),
        "dn_w_out": nrm(ks[14], (n_dn, DN_VAL_DIM, D_MODEL), DN_VAL_DIM ** -0.5),
        "swa_w_qkv": nrm(ks[15], (n_swa, D_MODEL, SWA_PROJ), D_MODEL ** -0.5),
        "swa_sinks": nrm(ks[16], (n_swa, SWA_Q_HEADS), 0.5),
        "swa_w_out": nrm(ks[17], (n_swa, SWA_Q_HEADS * SWA_HEAD_DIM, D_MODEL), (SWA_Q_HEADS * SWA_HEAD_DIM) ** -0.5),
        "final_norm": gain(ks[18], (D_MODEL,)),
    }


def reference(x, meta_tokens, ffn_pre_norm, ffn_pre_w_gu, ffn_pre_w_down, mix_norm,
              ffn_post_norm, ffn_post_w_gu, ffn_post_w_down, dn_w_in, dn_conv_w, dn_a_log,
              dn_dt_bias, dn_out_norm, dn_w_out, swa_w_qkv, swa_sinks, swa_w_out, final_norm):
    B = x.shape[0]
    meta = jnp.broadcast_to(meta_tokens[None].astype(x.dtype), (B, N_META, x.shape[-1]))
    h = jnp.concatenate([meta, x], axis=1)
    for i in range(DEPTH):
        j = i // N_MIXERS
        h = h + FFN_RES * swiglu(rms_norm(h, ffn_pre_norm[i]), ffn_pre_w_gu[i], ffn_pre_w_down[i])
        hn = rms_norm(h, mix_norm[i])
        if i % N_MIXERS == 0:
            h = h + gated_deltanet(hn, dn_w_in[j], dn_conv_w[j], dn_a_log[j], dn_dt_bias[j],
                                   dn_out_norm[j], dn_w_out[j])
        else:
            h = h + sliding_window_attention(hn, swa_w_qkv[j], swa_sinks[j], swa_w_out[j])
        h = h + FFN_RES * swiglu(rms_norm(h, ffn_post_norm[i]), ffn_post_w_gu[i], ffn_post_w_down[i])
    return rms_norm(h[:, N_META:], final_norm)
```

```python
import numpy as np
import concourse.bass as bass
import concourse.mybir as mybir
from concourse.bass_utils import run_bass_kernel_spmd

F32 = mybir.dt.float32
BF16 = mybir.dt.bfloat16
AF = mybir.ActivationFunctionType
ALU = mybir.AluOpType

D = 2048
SEQ = 2048
NMETA = 16
NT = 17
NTP = NT * 128
DFF = 5504
NFC = DFF // 128
KC = D // 128
DEPTH = 4
EPS = 1e-6
SB_BYTES = 212800
TG = [(0, 512), (512, 512), (1024, 512), (1536, 512), (2048, 128)]

ENGS = ("pe", "act", "dve", "pool", "sp")


class Buf:
    __slots__ = ("name", "w", "r", "rd")

    def __init__(self, name=""):
        self.name = name
        self.w = None
        self.r = {}
        self.rd = []


class Op:
    __slots__ = ("eng", "fn", "deps", "is_dma", "dsem", "dval", "signal", "sigidx", "attach")


class Prog:
    def __init__(self, nc):
        self.nc = nc
        self.ops = {e: [] for e in ENGS}
        self.dsems = {}
        self.nops = 0

    def dma_sem(self, name):
        if name not in self.dsems:
            self.dsems[name] = [None, 0]
        return name

    def op(self, eng, fn, reads=(), writes=(), dsem=None, lhs=None):
        o = Op()
        o.attach = None
        if lhs is not None and lhs.w is not None and (lhs.w.is_dma or lhs.w.eng != eng):
            o.attach = lhs.w
        o.eng = eng
        o.fn = fn
        o.is_dma = dsem is not None
        o.signal = False
        o.sigidx = 0
        deps = set()
        for b in reads:
            if b.w is not None:
                deps.add(b.w)
        for b in writes:
            if b.w is not None:
                deps.add(b.w)
            for r in b.r.values():
                deps.add(r)
            for r in b.rd:
                deps.add(r)
        for b in reads:
            if o.is_dma:
                b.rd.append(o)
            else:
                b.r[eng] = o
        for b in writes:
            b.w = o
            b.r = {}
            b.rd = []
        o.deps = [d for d in deps if d.is_dma or not (d.eng == eng and eng == "pe")]
        for d in o.deps:
            if not d.is_dma:
                d.signal = True
        if o.is_dma:
            s = self.dsems.setdefault(dsem, [None, 0])
            s[1] += 16
            o.dsem = dsem
            o.dval = s[1]
        self.ops[eng].append(o)
        self.nops += 1
        return o

    def barrier(self):
        lasts = []
        for e in ENGS:
            last = None
            for o in reversed(self.ops[e]):
                if not o.is_dma and o.fn is not None:
                    last = o
                    break
            if last is not None:
                last.signal = True
                lasts.append(last)
        dlast = {}
        for e in ENGS:
            for o in self.ops[e]:
                if o.is_dma:
                    dlast[o.dsem] = o
        for e in ENGS:
            o = Op()
            o.eng = e
            o.fn = None
            o.is_dma = False
            o.signal = False
            o.sigidx = 0
            o.deps = [l for l in lasts if l.eng != e] + list(dlast.values())
            o.attach = None
            self.ops[e].append(o)

    def emit(self):
        nc = self.nc
        import contextlib
        with contextlib.ExitStack() as st:
            esem = {e: st.enter_context(nc.semaphore("s_" + e)) for e in ENGS}
            for name, s in self.dsems.items():
                s[0] = st.enter_context(nc.semaphore("d_" + name))
            for e in ENGS:
                c = 0
                for o in self.ops[e]:
                    if o.is_dma or o.fn is None:
                        continue
                    if o.signal:
                        c += 1
                        o.sigidx = c
            block = st.enter_context(nc.Block())
            dsems = self.dsems

            def replay(e, engobj):
                waited = {}

                def semval(d):
                    if d.is_dma:
                        return ("d", d.dsem), dsems[d.dsem][0], d.dval
                    return ("e", d.eng), esem[d.eng], d.sigidx

                for o in self.ops[e]:
                    att = o.attach
                    need = {}
                    for d in o.deps:
                        if d is att:
                            continue
                        key, sem, val = semval(d)
                        if waited.get(key, 0) >= val:
                            continue
                        if key not in need or need[key][1] < val:
                            need[key] = (sem, val)
                    pend = []
                    for key, (sem, val) in need.items():
                        waited[key] = val
                        pend.append((sem, val))
                    if att is None and pend and o.fn is not None and not o.is_dma:
                        last = pend.pop()
                    else:
                        last = None
                    for sem, val in pend:
                        engobj.wait_ge(sem, val)
                    if o.fn is None:
                        continue
                    ins = o.fn(engobj)
                    if att is not None:
                        key, sem, val = semval(att)
                        if waited.get(key, 0) < val:
                            waited[key] = val
                        ins._wait_ge(sem, val)
                    elif last is not None:
                        ins._wait_ge(last[0], last[1])
                    if o.is_dma:
                        ins.then_inc(dsems[o.dsem][0], 16)
                    elif o.signal:
                        ins.then_inc(esem[e], 1)

            @block.tensor
            def _(t):
                replay("pe", t)

            @block.scalar
            def _(t):
                replay("act", t)

            @block.vector
            def _(t):
                replay("dve", t)

            @block.gpsimd
            def _(t):
                replay("pool", t)

            @block.sync
            def _(t):
                replay("sp", t)


class Arena:
    def __init__(self, big, nbytes):
        self.big = big
        self.nbytes = nbytes
        self.top = 0
        self.marks = []

    def push(self):
        self.marks.append(self.top)

    def pop(self):
        self.top = self.marks.pop()

    def alloc(self, free_elems, dtype):
        esz = 4 if dtype == F32 else 2
        nb = free_elems * esz
        nb = (nb + 63) // 64 * 64
        off = self.top
        self.top += nb
        assert self.top <= self.nbytes, f"SBUF arena overflow {self.top} > {self.nbytes}"
        v = self.big[:, off // 2:(off + free_elems * esz) // 2]
        if dtype == F32:
            v = v.bitcast(F32)
        return v


class K:
    pass


def _mm(out, lhsT, rhs, start, stop):
    return lambda e: e.matmul(out, lhsT, rhs, start=start, stop=stop)


def _tr(out, in_, ident):
    return lambda e: e.transpose(out, in_, ident)


def _act(out, in_, func, **kw):
    return lambda e: e.activation(out=out, in_=in_, func=func, **kw)


def _tt(out, in0, in1, op):
    return lambda e: e.tensor_tensor(out=out, in0=in0, in1=in1, op=op)


def _ts(out, in0, s1, s2, op0, op1=None, **kw):
    if op1 is None:
        return lambda e: e.tensor_scalar(out=out, in0=in0, scalar1=s1, scalar2=s2, op0=op0, **kw)
    return lambda e: e.tensor_scalar(out=out, in0=in0, scalar1=s1, scalar2=s2, op0=op0, op1=op1, **kw)


def _stt(out, in0, scalar, in1, op0, op1, **kw):
    return lambda e: e.scalar_tensor_tensor(out=out, in0=in0, scalar=scalar, in1=in1, op0=op0, op1=op1, **kw)


def _copy(out, in_):
    return lambda e: e.tensor_copy(out=out, in_=in_)


def _memset(ap, v):
    return lambda e: e.memset(ap, v)


def _dma(out, in_):
    return lambda e: e.dma_start(out=out, in_=in_)


def dump(k, name, ap, reads):
    if not getattr(k, "debug", False):
        return
    shp = list(ap.shape)
    dt = k.nc.dram_tensor("dbg_" + name, shp, ap.dtype, kind="ExternalOutput").ap()
    k.p.op("sp", _dma(dt, ap), reads, [Buf()], dsem="dbg_" + name)


def norm_pass(k, gain_row, xnT, xnT_bufs, skip_tile0_pad=False):
    p, ar = k.p, k.ar
    ar.push()
    gain_bc = ar.alloc(D, F32)
    gb = Buf("gain")
    p.op("sp", _dma(gain_bc, gain_row.partition_broadcast(128)), [], [gb], dsem="gain")
    NH = 4
    hin = [ar.alloc(D, F32) for _ in range(NH)]
    hinb = [Buf(f"hin{i_}") for i_ in range(NH)]
    junk = ar.alloc(D, BF16)
    junkb = Buf("junk")
    xn = [ar.alloc(D, BF16) for _ in range(2)]
    xnb = [Buf("xn0"), Buf("xn1")]
    ss = [ar.alloc(1, F32) for _ in range(2)]
    ssb = [Buf(), Buf()]
    rs = [ar.alloc(1, F32) for _ in range(2)]
    rsb = [Buf(), Buf()]
    for t in range(NT):
        s = t % 2
        sh = t % NH
        p.op("sp", _dma(hin[sh], k.h[t * 128:(t + 1) * 128, :]), [k.hb[t]], [hinb[sh]], dsem=f"hin{sh}")
        p.op("act", _act(junk, hin[sh], AF.Square, accum_out=ss[s]), [hinb[sh]], [junkb, ssb[s]])
        p.op("act", _act(ss[s], ss[s], AF.Sqrt, scale=1.0 / D, bias=k.eps_ap), [ssb[s], k.epsb], [ssb[s]])
        p.op("dve", lambda e, o=rs[s], i=ss[s]: e.reciprocal(out=o, in_=i), [ssb[s]], [rsb[s]])
        p.op("dve", _stt(xn[s], hin[sh], rs[s][:, 0:1], gain_bc, ALU.mult, ALU.mult),
             [hinb[sh], rsb[s], gb], [xnb[s]])
        for q4 in range(4):
            pst = k.psum[4 + q4]
            pb = k.psb[4 + q4]
            for j in range(4):
                kc = q4 * 4 + j
                p.op("pe", _mm(pst[:, j * 128:(j + 1) * 128], xn[s][:, kc * 128:(kc + 1) * 128], k.ident, True, True),
                     [xnb[s], k.identb], [pb], lhs=xnb[s])
            dst = xnT[:, q4 * 4:(q4 + 1) * 4, t * 128:(t + 1) * 128]
            src = pst.rearrange("p (a b) -> p a b", a=4)
            if q4 % 2 == 0:
                p.op("act", lambda e, dst=dst, src=src: e.copy(out=dst, in_=src), [pb], [xnT_bufs[t][q4]])
            else:
                p.op("dve", _copy(dst, src), [pb], [xnT_bufs[t][q4]])
    p.barrier()
    ar.pop()


def ffn_stage(k, gain_row, w_gu, w_down, scale=0.5):
    p, ar = k.p, k.ar
    ar.push()
    xnT = ar.alloc(KC * NTP, BF16).rearrange("p (a b) -> p a b", a=KC)
    xnTb = [[Buf(f"xnT{t}_{q}") for q in range(4)] for t in range(NT)]
    norm_pass(k, gain_row, xnT, xnTb)

    GC = 8
    ng = (NFC + GC - 1) // GC
    groups = []
    c = 0
    for gi_ in range(ng):
        n = NFC // ng + (1 if gi_ < NFC % ng else 0)
        groups.append((c, n))
        c += n
    hT = ar.alloc(GC * NTP, BF16).rearrange("p (a b) -> p a b", a=GC)
    hTb = [[Buf(f"hT{ci}_{t}") for t in range(NT)] for ci in range(GC)]
    wd = [ar.alloc(D, BF16) for _ in range(GC)]
    wdb = [Buf(f"wd{ci}") for ci in range(GC)]
    NWS = 2
    wg = [ar.alloc(KC * 256, BF16).rearrange("p (a b) -> p a b", a=KC) for _ in range(NWS)]
    wu = [ar.alloc(KC * 256, BF16).rearrange("p (a b) -> p a b", a=KC) for _ in range(NWS)]
    wgb = [Buf(f"wg{i}") for i in range(NWS)]
    wub = [Buf(f"wu{i}") for i in range(NWS)]
    sg = [ar.alloc(512, F32) for _ in range(2)]
    sgb = [Buf("sg0"), Buf("sg1")]
    hio = [ar.alloc(D, F32) for _ in range(4)]
    wgu_v = w_gu.rearrange("(kc p) n -> p kc n", p=128)

    hioq = [[Buf(f"hio{s}_{j}") for j in range(4)] for s in range(4)]
    pair_idx = 0
    tgi = 0
    tile_ctr = 0
    n_tiles_total = len(groups) * NT
    issued = [0]

    def ensure_loads(upto):
        while issued[0] <= min(upto, n_tiles_total - 1):
            n_ = issued[0]
            t_, s_ = n_ % NT, n_ % 4
            p.op("sp", _dma(hio[s_], k.h[t_ * 128:(t_ + 1) * 128, :]), [k.hb[t_]], hioq[s_], dsem=f"hio{s_}")
            issued[0] += 1

    for (c0, gn) in groups:
        ci = 0
        first = True
        while ci < gn:
            npair = min(2, gn - ci)
            s = pair_idx % NWS
            pair_idx += 1
            cc = c0 + ci
            p.op("pool", _dma(wg[s][:, :, 0:npair * 128], wgu_v[:, :, cc * 128:(cc + npair) * 128]), [], [wgb[s]], dsem=f"wg{s}")
            p.op("pool", _dma(wu[s][:, :, 0:npair * 128], wgu_v[:, :, DFF + cc * 128:DFF + (cc + npair) * 128]), [], [wub[s]], dsem=f"wu{s}")
            if first:
                first = False
                for cj in range(gn):
                    p.op("pool", _dma(wd[cj], w_down[(c0 + cj) * 128:(c0 + cj + 1) * 128, :]), [], [wdb[cj]], dsem=f"wd{cj}")
            for q in range(npair):
                for (t0, tn) in TG:
                    b2 = tgi % 2
                    tgi += 1
                    psG, psU = k.psum[b2 * 2], k.psum[b2 * 2 + 1]
                    pGb, pUb = k.psb[b2 * 2], k.psb[b2 * 2 + 1]
                    xr = [b for t in range(t0 // 128, (t0 + tn) // 128) for b in xnTb[t]]
                    for kc in range(KC):
                        p.op("pe", _mm(psG[:, 0:tn], wg[s][:, kc, q * 128:(q + 1) * 128], xnT[:, kc, t0:t0 + tn], kc == 0, kc == KC - 1),
                             [wgb[s]] + xr, [pGb], lhs=wgb[s])
                    for kc in range(KC):
                        p.op("pe", _mm(psU[:, 0:tn], wu[s][:, kc, q * 128:(q + 1) * 128], xnT[:, kc, t0:t0 + tn], kc == 0, kc == KC - 1),
                             [wub[s]] + xr, [pUb], lhs=wub[s])
                    p.op("act", _act(sg[b2][:, 0:tn], psG[:, 0:tn], AF.Silu), [pGb], [sgb[b2]])
                    p.op("dve", _tt(hT[:, ci + q, t0:t0 + tn], sg[b2][:, 0:tn], psU[:, 0:tn], ALU.mult),
                         [sgb[b2], pUb], [hTb[ci + q][t] for t in range(t0 // 128, (t0 + tn) // 128)])
            ci += npair
        for t in range(NT):
            s = tile_ctr % 4
            ensure_loads(tile_ctr + 2)
            tile_ctr += 1
            pb0 = 4 if tile_ctr % 2 == 1 else 0
            for ci in range(gn):
                for j in range(4):
                    wr = [k.psb[pb0 + jj] for jj in range(4)] if (ci == 0 and j == 0) else [k.psb[pb0 + j]]
                    p.op("pe", _mm(k.psum[pb0 + j][:, :], hT[:, ci, t * 128:(t + 1) * 128], wd[ci][:, j * 512:(j + 1) * 512], ci == 0, ci == gn - 1),
                         [hTb[ci][t], wdb[ci]], wr, lhs=hTb[ci][t])
            for j in range(4):
                p.op("dve", _stt(hio[s][:, j * 512:(j + 1) * 512], k.psum[pb0 + j][:, :], scale, hio[s][:, j * 512:(j + 1) * 512], ALU.mult, ALU.add),
                     [k.psb[pb0 + j], hioq[s][j]], [hioq[s][j]])
            p.op("sp", _dma(k.h[t * 128:(t + 1) * 128, :], hio[s]), hioq[s], [k.hb[t]], dsem=f"h{t}")
    p.barrier()
    ar.pop()


def out_proj(k, actT, act_bufs, n, w_rows, scale, dsem_prefix="op"):
    p, ar = k.p, k.ar
    ar.push()
    w = ar.alloc(n * D, BF16).rearrange("p (a b) -> p a b", a=n)
    wb = [Buf(f"opw{ci}") for ci in range(n)]
    for ci in range(n):
        p.op("pool", _dma(w[:, ci, :], w_rows[ci * 128:(ci + 1) * 128, :]), [], [wb[ci]], dsem=f"{dsem_prefix}w{ci % 4}")
    hio = [ar.alloc(D, F32) for _ in range(4)]
    hioq = [[Buf(f"ophio{s}_{j}") for j in range(4)] for s in range(4)]
    issued = [0]

    def ensure_loads(upto):
        while issued[0] <= min(upto, NT - 1):
            t_ = issued[0]
            p.op("sp", _dma(hio[t_ % 4], k.h[t_ * 128:(t_ + 1) * 128, :]), [k.hb[t_]], hioq[t_ % 4], dsem=f"hio{t_ % 4}")
            issued[0] += 1

    for t in range(NT):
        s = t % 4
        ensure_loads(t + 2)
        pb0 = 4 if t % 2 == 1 else 0
        for ci in range(n):
            for j in range(4):
                wr = [k.psb[pb0 + jj] for jj in range(4)] if (ci == 0 and j == 0) else [k.psb[pb0 + j]]
                p.op("pe", _mm(k.psum[pb0 + j][:, :], actT[:, ci, t * 128:(t + 1) * 128], w[:, ci, j * 512:(j + 1) * 512], ci == 0, ci == n - 1),
                     [act_bufs[ci][t], wb[ci]], wr, lhs=act_bufs[ci][t])
        for j in range(4):
            p.op("dve", _stt(hio[s][:, j * 512:(j + 1) * 512], k.psum[pb0 + j][:, :], scale, hio[s][:, j * 512:(j + 1) * 512], ALU.mult, ALU.add),
                 [k.psb[pb0 + j], hioq[s][j]], [hioq[s][j]])
        if t == 0:
            p.op("sp", _dma(k.h[112:128, :], hio[s][112:128, :]), hioq[s], [k.hb[t]], dsem=f"h{t}")
        else:
            p.op("sp", _dma(k.h[t * 128:(t + 1) * 128, :], hio[s]), hioq[s], [k.hb[t]], dsem=f"h{t}")
    p.barrier()
    ar.pop()


def swa_stage(k, gain_row, w_qkv, sinks2, w_out):
    p, ar = k.p, k.ar
    ar.push()
    xnT = ar.alloc(KC * NTP, BF16).rearrange("p (a b) -> p a b", a=KC)
    xnTb = [[Buf(f"xnT{t}_{q}") for q in range(4)] for t in range(NT)]
    norm_pass(k, gain_row, xnT, xnTb)
    qT = ar.alloc(16 * NTP, BF16).rearrange("p (a b) -> p a b", a=16)
    kT = ar.alloc(4 * NTP, BF16).rearrange("p (a b) -> p a b", a=4)
    qkb = [Buf(f"qk{c}") for c in range(20)]
    ar.push()
    cosT = ar.alloc(NTP, F32)
    sinT = ar.alloc(NTP, F32)
    tabb = Buf("tab")
    p.op("sp", _dma(cosT, k.c_cos[:, :]), [], [tabb], dsem="tab")
    p.op("sp", _dma(sinT, k.c_sin[:, :]), [], [tabb], dsem="tab")
    wA = [ar.alloc(KC * 128, BF16).rearrange("p (a b) -> p a b", a=KC) for _ in range(2)]
    wP = [ar.alloc(KC * 128, BF16).rearrange("p (a b) -> p a b", a=KC) for _ in range(2)]
    wAb = [Buf("wA0"), Buf("wA1")]
    wPb = [Buf("wP0"), Buf("wP1")]
    t1 = [ar.alloc(512, F32) for _ in range(2)]
    t2 = [ar.alloc(512, F32) for _ in range(2)]
    t1b = [Buf(), Buf()]
    t2b = [Buf(), Buf()]
    wq_v = w_qkv.rearrange("(kc p) n -> p kc n", p=128)
    tgi = 0
    for c in range(20):
        s = c % 2
        if c < 16:
            base = c * 128
            piecesA = [(0, base, 128)]
            piecesP = [(0, base + 32, 32), (32, base, 32), (64, base + 96, 32), (96, base + 64, 32)]
            dst = qT[:, c, :]
        else:
            base = 2048 + (c - 16) * 64
            piecesA = [(0, base, 64), (64, base, 64)]
            piecesP = [(0, base + 32, 32), (32, base, 32), (64, base + 32, 32), (96, base, 32)]
            dst = kT[:, c - 16, :]
        for (d0, s0, n) in piecesA:
            p.op("pool", _dma(wA[s][:, :, d0:d0 + n], wq_v[:, :, s0:s0 + n]), [], [wAb[s]], dsem=f"wA{s}")
        srcv = wA[s].rearrange("p k (h f c) -> p k h f c", h=2, f=2)
        dstv = wP[s].rearrange("p k (h f c) -> p k h f c", h=2, f=2)
        for f_ in range(2):
            p.op("act", lambda e, o=dstv[:, :, :, f_, :], i_=srcv[:, :, :, 1 - f_, :]: e.copy(out=o, in_=i_), [wAb[s]], [wPb[s]])
        for (t0, tn) in TG:
            b2 = tgi % 2
            tgi += 1
            psA, psP = k.psum[b2 * 2], k.psum[b2 * 2 + 1]
            pAb, pPb = k.psb[b2 * 2], k.psb[b2 * 2 + 1]
            xr = [b for t in range(t0 // 128, (t0 + tn) // 128) for b in xnTb[t]]
            for kc in range(KC):
                p.op("pe", _mm(psA[:, 0:tn], wA[s][:, kc, :], xnT[:, kc, t0:t0 + tn], kc == 0, kc == KC - 1), [wAb[s]] + xr, [pAb], lhs=wAb[s])
            for kc in range(KC):
                p.op("pe", _mm(psP[:, 0:tn], wP[s][:, kc, :], xnT[:, kc, t0:t0 + tn], kc == 0, kc == KC - 1), [wPb[s]] + xr, [pPb], lhs=wPb[s])
            p.op("dve", _tt(t1[b2][:, 0:tn], psA[:, 0:tn], cosT[:, t0:t0 + tn], ALU.mult), [pAb, tabb], [t1b[b2]])
            p.op("dve", _tt(t2[b2][:, 0:tn], psP[:, 0:tn], sinT[:, t0:t0 + tn], ALU.mult), [pPb, tabb], [t2b[b2]])
            p.op("pool", _tt(dst[:, t0:t0 + tn], t1[b2][:, 0:tn], t2[b2][:, 0:tn], ALU.add), [t1b[b2], t2b[b2]], [qkb[c]])
    p.barrier()
    ar.pop()
    vp_flat = ar.alloc(NT * 2 * 4 * 128, BF16)
    vp = vp_flat.rearrange("p (t h g c) -> p t h g c", t=NT, h=2, g=4)
    vpb = [Buf(f"vp{t}") for t in range(NT)]
    p.op("pool", _memset(vp_flat, 0.0), [], vpb)
    ar.push()
    wv = ar.alloc(KC * 256, BF16).rearrange("p (a b) -> p a b", a=KC)
    wvb = Buf("wv")
    p.op("pool", _dma(wv, wq_v[:, :, 2304:2560]), [], [wvb], dsem="wv")
    for t in range(NT):
        b2 = t % 2
        psV, pVb = k.psum[4 + b2], k.psb[4 + b2]
        for kc in range(KC):
            p.op("pe", _mm(psV[:, 0:256], xnT[:, kc, t * 128:(t + 1) * 128], wv[:, kc, :], kc == 0, kc == KC - 1),
                 xnTb[t] + [wvb], [pVb], lhs=xnTb[t][kc // 4])
        src = psV[:, 0:256].rearrange("p (g c) -> p g c", g=4)
        p.op("act", lambda e, o=vp[:, t, 0, :, 0:64], i=src: e.copy(out=o, in_=i), [pVb], [vpb[t]])
        p.op("dve", _copy(vp[:, t, 1, :, 64:128], src), [pVb], [vpb[t]])
    p.barrier()
    ar.pop()
    OT = xnT
    OTb = [[Buf(f"OT{c}_{t}") for t in range(NT)] for c in range(16)]
    mask3 = ar.alloc(384, BF16)
    mask0 = ar.alloc(128, BF16)
    ones_p = ar.alloc(2 * 128, BF16).rearrange("p (h c) -> p h c", h=2)
    onesm_p = ar.alloc(2 * 128, BF16).rearrange("p (h c) -> p h c", h=2)
    es = ar.alloc(16, F32)
    cb = Buf("swaconst")
    p.op("pool", _dma(mask3, k.c_mask3[:, :]), [], [cb], dsem="const")
    p.op("pool", _dma(mask0, k.c_mask0[:, :]), [], [cb], dsem="const")
    p.op("pool", _dma(ones_p.rearrange("p h c -> p (h c)"), k.c_onesp[:, :]), [], [cb], dsem="const")
    p.op("pool", _dma(onesm_p.rearrange("p h c -> p (h c)"), k.c_onesmp[:, :]), [], [cb], dsem="const")
    esb = Buf("es")
    for hf in range(2):
        p.op("sp", _dma(es[hf * 64:(hf + 1) * 64, :], sinks2[hf, :].partition_broadcast(64)), [], [esb], dsem="es")
    p.op("act", _act(es, es, AF.Exp), [esb], [esb])
    ex = [[ar.alloc(384, BF16) for _ in range(2)] for _ in range(2)]
    exb = [[Buf(), Buf()], [Buf(), Buf()]]
    PT = [[ar.alloc(384, BF16) for _ in range(2)] for _ in range(2)]
    PTb = [[Buf(), Buf()], [Buf(), Buf()]]
    dn = [ar.alloc(128, F32) for _ in range(2)]
    dnb = [Buf(), Buf()]
    it = 0
    for c in range(16):
        g = (2 * c) // 8
        for qt in range(NT):
            b2 = it % 2
            it += 1
            if qt == 0:
                kbs = [(0, "m0")]
            elif qt == 1:
                kbs = [(1, "cur"), (0, "meta")]
            else:
                kbs = [(qt, "cur"), (0, "meta"), (qt - 1, "prev")]
            nk = len(kbs)
            for hf in range(2):
                psS, pSb = k.psum[b2 * 2 + hf], k.psb[b2 * 2 + hf]
                rows = slice(hf * 64, (hf + 1) * 64)
                for bi, (kt, kind) in enumerate(kbs):
                    p.op("pe", _mm(psS[:, bi * 128:(bi + 1) * 128], kT[rows, g, kt * 128:(kt + 1) * 128], qT[rows, c, qt * 128:(qt + 1) * 128], True, True),
                         [qkb[16 + g], qkb[c]], [pSb], lhs=qkb[16 + g])
                p.op("act", _act(ex[b2][hf][:, 0:nk * 128], psS[:, 0:nk * 128], AF.Exp, scale=0.125), [pSb], [exb[b2][hf]])
                mk = mask0 if qt == 0 else mask3[:, 0:nk * 128]
                p.op("dve", _tt(PT[b2][hf][:, 0:nk * 128], ex[b2][hf][:, 0:nk * 128], mk, ALU.mult), [exb[b2][hf], cb], [PTb[b2][hf]])
            psO, pOb = k.psum[4 + b2], k.psb[4 + b2]
            psDn, pDb = k.psum[6 + b2], k.psb[6 + b2]
            nmm = 2 * nk
            i = 0
            for hf in range(2):
                for bi, (kt, kind) in enumerate(kbs):
                    p.op("pe", _mm(psO[:, 0:128], vp[:, kt, hf, g, :], PT[b2][hf][:, bi * 128:(bi + 1) * 128], i == 0, i == nmm - 1),
                         [vpb[kt], PTb[b2][hf]], [pOb], lhs=vpb[kt])
                    i += 1
            i = 0
            for hf in range(2):
                for bi, (kt, kind) in enumerate(kbs):
                    on = onesm_p if kind in ("meta", "m0") else ones_p
                    p.op("pe", _mm(psDn[:, 0:128], on[:, hf, :], PT[b2][hf][:, bi * 128:(bi + 1) * 128], i == 0, i == nmm - 1),
                         [cb, PTb[b2][hf]], [pDb], lhs=cb)
                    i += 1
            p.op("act", _act(dn[b2], psDn[:, 0:128], AF.Ln, bias=es[:, c:c + 1]), [pDb, esb], [dnb[b2]])
            p.op("act", _act(dn[b2], dn[b2], AF.Exp, scale=-1.0), [dnb[b2]], [dnb[b2]])
            p.op("dve", _tt(OT[:, c, qt * 128:(qt + 1) * 128], psO[:, 0:128], dn[b2], ALU.mult), [pOb, dnb[b2]], [OTb[c][qt]])
    p.barrier()
    ar.pop()
    ar.push()
    OT2 = ar.alloc(KC * NTP, BF16).rearrange("p (a b) -> p a b", a=KC)
    out_proj(k, OT2, [OTb[c] for c in range(16)], 16, w_out, 1.0)
    ar.pop()


def _bc(ap2, n=128):
    return ap2.unsqueeze(2).broadcast_to([128, ap2.shape[1], n])


def _bo(ap2, m):
    return ap2.unsqueeze(1).broadcast_to([128, m, ap2.shape[1]])


def _v4(ap):
    return ap.rearrange("p (h c) -> p h c", c=128)


def _rep2(ap256):
    return ap256.rearrange("p (q c) -> p q c", q=2).unsqueeze(2).broadcast_to([128, 2, 2, 128])


def _bc22(ap4):
    return ap4.rearrange("p (q r) -> p q r", q=2).unsqueeze(3).broadcast_to([128, 2, 2, 128])


def _v22(ap512):
    return ap512.rearrange("p (q r c) -> p q r c", q=2, r=2)


def dn_stage(k, i, j):
    ins = k.ins
    p, ar, nc = k.p, k.ar, k.nc
    gain_row = ins["mix_norm"][i, :]
    w_in = ins["dn_w_in"][j * D:(j + 1) * D, :].rearrange("(kc p) n -> p kc n", p=128)
    w_out = ins["dn_w_out"][j * 4096:(j + 1) * 4096, :]
    sfx = f"_{i}"
    dk_ = "ExternalOutput" if getattr(k, "debug", False) else "Internal"
    pfx = "dbg_" if getattr(k, "debug", False) else ""
    phases = getattr(k, "dn_phases", "PRO")
    qk_s = nc.dram_tensor(pfx + "qk_s" + sfx, [32, 128, NTP], BF16, kind=dk_).ap()
    v_s = nc.dram_tensor(pfx + "v_s" + sfx, [32, 128, NTP], BF16, kind=dk_).ap()
    z_s = nc.dram_tensor(pfx + "z_s" + sfx, [NTP, 4096], BF16, kind=dk_).ap()
    oT_s = nc.dram_tensor(pfx + "oT_s" + sfx, [32, 128, NTP], BF16, kind=dk_).ap()
    qksb = [Buf(f"qks{c}") for c in range(32)]
    vsb = [Buf(f"vs{c}") for c in range(32)]
    zsb = [Buf(f"zs{t}") for t in range(NT)]
    oTb = [[Buf(f"oTs{hv}_{t}") for t in range(NT)] for hv in range(32)]

    ar.push()
    g_all = ar.alloc(NT * 32, F32).rearrange("p (t h) -> p t h", t=NT)
    beta_all = ar.alloc(NT * 32, F32).rearrange("p (t h) -> p t h", t=NT)
    gab = [Buf(f"g{t}") for t in range(NT)]
    bab = [Buf(f"b{t}") for t in range(NT)]
    ones_f = ar.alloc(128, F32)
    onesb = Buf("ones")
    p.op("pool", _memset(ones_f, 1.0), [], [onesb])
    ar.push()
    xnT = ar.alloc(KC * NTP, BF16).rearrange("p (a b) -> p a b", a=KC)
    xnTb = [[Buf(f"xnT{t}_{q}") for q in range(4)] for t in range(NT)]
    norm_pass(k, gain_row, xnT, xnTb)
    xall = [b for t in range(NT) for b in xnTb[t]]
    cw = ar.alloc(64 * 4, F32).rearrange("p (c k) -> p c k", k=4)
    cwb = Buf("cw")
    p.op("sp", _dma(cw.rearrange("p c k -> p (c k)"), ins["dn_conv_w"][j * 128:(j + 1) * 128, :]), [], [cwb], dsem="cw")
    dtb = ar.alloc(32, F32)
    negA = ar.alloc(32, F32)
    smb = Buf("dnsmall")
    p.op("sp", _dma(dtb, ins["dn_dt_bias"][j, :].partition_broadcast(128)), [], [smb], dsem="cw")
    p.op("sp", _dma(negA, ins["dn_a_log"][j, :].partition_broadcast(128)), [], [smb], dsem="cw")
    p.op("act", _act(negA, negA, AF.Exp), [smb], [smb])
    p.op("dve", _ts(negA, negA, -1.0, None, ALU.mult), [smb], [smb])
    ws = [ar.alloc(KC * 256, BF16).rearrange("p (a b) -> p a b", a=KC) for _ in range(2)]
    wsb = [Buf("ws0"), Buf("ws1")]
    cbuf = [ar.alloc(NTP + 4, F32) for _ in range(2)]
    cbb = [Buf("cb0"), Buf("cb1")]
    for s_ in range(2):
        p.op("pool", _memset(cbuf[s_][:, 0:4], 0.0), [], [cbb[s_]])
    acc = [ar.alloc(NTP, F32) for _ in range(2)]
    accb = [[Buf()], [Buf()]]
    yb2 = [ar.alloc(NTP, F32) for _ in range(2)]
    ybb2 = [Buf("y0"), Buf("y1")]
    ysq2 = [ar.alloc(NTP, F32) for _ in range(2)]
    ysqb2 = [Buf("ysq0"), Buf("ysq1")]
    pending = []
    ybf = [ar.alloc(NTP, BF16) for _ in range(2)]
    ybfb = [Buf("ybf0"), Buf("ybf1")]
    tmpn = [ar.alloc(512, F32) for _ in range(2)]
    tmpnb = [Buf(), Buf()]
    CS = 1280
    bank = [0]

    def nb():
        b = bank[0]
        bank[0] = (b + 1) % 8
        return b

    for ch in range(64):
        s = (ch // 2) % 2
        if ch % 2 == 0:
            p.op("pool", _dma(ws[s], w_in[:, :, ch * 128:(ch + 2) * 128]), [], [wsb[s]], dsem=f"ws{s}")
        q = ch % 2
        cs = ch % 2
        for (t0, tn) in TG:
            b = nb()
            xr = [bb for t in range(t0 // 128, (t0 + tn) // 128) for bb in xnTb[t]]
            for kc in range(KC):
                p.op("pe", _mm(k.psum[b][:, 0:tn], ws[s][:, kc, q * 128:(q + 1) * 128], xnT[:, kc, t0:t0 + tn], kc == 0, kc == KC - 1),
                     [wsb[s]] + xr, [k.psb[b]], lhs=wsb[s])
            p.op("act", lambda e, o=cbuf[cs][:, 3 + t0:3 + t0 + tn], i_=k.psum[b][:, 0:tn]: e.copy(out=o, in_=i_), [k.psb[b]], [cbb[cs]])
        for hi, (eng, c0, c1) in enumerate((("dve", 0, NTP),)):
            for kk in range(4):
                src = cbuf[cs][:, c0 + kk:c1 + kk]
                if kk == 0:
                    p.op(eng, _ts(acc[cs][:, c0:c1], src, cw[:, ch, 0:1], None, ALU.mult), [cbb[cs], cwb], [accb[cs][hi]])
                else:
                    p.op(eng, _stt(acc[cs][:, c0:c1], src, cw[:, ch, kk:kk + 1], acc[cs][:, c0:c1], ALU.mult, ALU.add),
                         [cbb[cs], cwb, accb[cs][hi]], [accb[cs][hi]])
        if ch >= 32:
            p.op("act", _act(ybf[cs], acc[cs], AF.Silu), accb[cs], [ybfb[cs]])
            p.op("sp", _dma(v_s[ch - 32], ybf[cs]), [ybfb[cs]], [vsb[ch - 32]], dsem=f"ybf{cs}")
        else:
            yb, ybb, ysq, ysqb = yb2[cs], ybb2[cs], ysq2[cs], ysqb2[cs]
            p.op("act", _act(yb, acc[cs], AF.Silu), accb[cs], [ybb])
            p.op("act", _act(ysq, yb, AF.Square), [ybb], [ysqb])

            def l2tail(ch=ch, cs=cs, yb=yb, ybb=ybb, ysq=ysq, ysqb=ysqb):
                sc = (128.0 ** -0.5) if ch < 16 else 1.0
                for ti, (t0, tn) in enumerate(TG):
                    b = nb()
                    p.op("pe", _mm(k.psum[b][:, 0:tn], ones_f, ysq[:, t0:t0 + tn], True, True), [onesb, ysqb], [k.psb[b]], lhs=onesb)
                    p.op("act", _act(tmpn[ti % 2][:, 0:tn], k.psum[b][:, 0:tn], AF.Ln, bias=k.eps_ap), [k.psb[b], k.epsb], [tmpnb[ti % 2]])
                    p.op("act", _act(tmpn[ti % 2][:, 0:tn], tmpn[ti % 2][:, 0:tn], AF.Exp, scale=-0.5), [tmpnb[ti % 2]], [tmpnb[ti % 2]])
                    p.op("dve", _stt(ybf[cs][:, t0:t0 + tn], yb[:, t0:t0 + tn], sc, tmpn[ti % 2][:, 0:tn], ALU.mult, ALU.mult),
                         [ybb, tmpnb[ti % 2]], [ybfb[cs]])
                p.op("sp", _dma(qk_s[ch], ybf[cs]), [ybfb[cs]], [qksb[ch]], dsem=f"ybf{cs}")
            pending.append(l2tail)
        while len(pending) > (1 if ch < 31 else 0):
            pending.pop(0)()
    wz = [ar.alloc(KC * 512, BF16).rearrange("p (a b) -> p a b", a=KC) for _ in range(2)]
    wzb = [Buf("wz0"), Buf("wz1")]
    zst = [ar.alloc(512, BF16) for _ in range(2)]
    zstb = [Buf(), Buf()]
    zi = 0
    for cg in range(8):
        s = cg % 2
        p.op("pool", _dma(wz[s], w_in[:, :, 8192 + cg * 512:8192 + (cg + 1) * 512]), [], [wzb[s]], dsem=f"wz{s}")
        for t in range(NT):
            b = nb()
            for kc in range(KC):
                p.op("pe", _mm(k.psum[b][:, :], xnT[:, kc, t * 128:(t + 1) * 128], wz[s][:, kc, :], kc == 0, kc == KC - 1),
                     xnTb[t] + [wzb[s]], [k.psb[b]], lhs=xnTb[t][kc // 4])
            zz = zi % 2
            zi += 1
            p.op("act", _act(zst[zz], k.psum[b][:, :], AF.Silu), [k.psb[b]], [zstb[zz]])
            p.op("sp", _dma(z_s[t * 128:(t + 1) * 128, cg * 512:(cg + 1) * 512], zst[zz]), [zstb[zz]], [zsb[t]], dsem=f"zst{zz}")
    wba = ar.alloc(KC * 64, BF16).rearrange("p (a b) -> p a b", a=KC)
    wbab = Buf("wba")
    p.op("pool", _dma(wba, w_in[:, :, 12288:12352]), [], [wbab], dsem="wba")
    ta = [ar.alloc(32, F32) for _ in range(2)]
    tab_ = [Buf(), Buf()]
    for t in range(NT):
        b = nb()
        for kc in range(KC):
            p.op("pe", _mm(k.psum[b][:, 0:64], xnT[:, kc, t * 128:(t + 1) * 128], wba[:, kc, :], kc == 0, kc == KC - 1),
                 xnTb[t] + [wbab], [k.psb[b]], lhs=xnTb[t][kc // 4])
        p.op("act", _act(beta_all[:, t, :], k.psum[b][:, 0:32], AF.Sigmoid), [k.psb[b]], [bab[t]])
        p.op("dve", _tt(ta[t % 2], k.psum[b][:, 32:64], dtb, ALU.add), [k.psb[b], smb], [tab_[t % 2]])
        p.op("act", _act(ta[t % 2], ta[t % 2], AF.Exp), [tab_[t % 2]], [tab_[t % 2]])
        p.op("act", _act(ta[t % 2], ta[t % 2], AF.Ln, bias=1.0), [tab_[t % 2]], [tab_[t % 2]])
        p.op("dve", _tt(g_all[:, t, :], ta[t % 2], negA, ALU.mult), [tab_[t % 2], smb], [gab[t]])
    dump(k, "g_all" + sfx, g_all.rearrange("p t h -> p (t h)"), gab)
    dump(k, "beta_all" + sfx, beta_all.rearrange("p t h -> p (t h)"), bab)
    p.barrier()
    ar.pop()
    if "R" not in phases:
        ar.pop()
        return
    ar.push()
    ident_f = ar.alloc(128, F32)
    ui_f = ar.alloc(128, F32)
    sl_f = ar.alloc(128, F32)
    og_bc = ar.alloc(128, F32)
    cfb = Buf("dnconst")
    p.op("sp", _dma(ident_f, k.ident_dram[:, :]), [], [cfb], dsem="cw")
    p.op("sp", _dma(ui_f, k.c_ui[:, :]), [], [cfb], dsem="cw")
    p.op("sp", _dma(sl_f, k.c_sl[:, :]), [], [cfb], dsem="cw")
    p.op("sp", _dma(og_bc, ins["dn_out_norm"][j, :].partition_broadcast(128)), [], [cfb], dsem="cw")
    S_f = ar.alloc(32 * 128, F32)
    S_b = ar.alloc(32 * 128, BF16)
    Sfb = [Buf(f"Sf{g}") for g in range(8)]
    Sbb = [Buf(f"Sb{g}") for g in range(8)]
    p.op("pool", _memset(S_f, 0.0), [], Sfb)
    p.op("pool", _memset(S_b, 0.0), [], Sbb)
    qkT = [ar.alloc(32 * 128, BF16).rearrange("p (h c) -> p h c", h=32) for _ in range(2)]
    vT = [ar.alloc(32 * 128, BF16).rearrange("p (h c) -> p h c", h=32) for _ in range(2)]
    zt = [ar.alloc(4096, BF16) for _ in range(2)]
    qkTb = [Buf("qkT0"), Buf("qkT1")]
    vTb = [Buf("vT0"), Buf("vT1")]
    ztb = [Buf("zt0"), Buf("zt1")]
    NSC = 7
    gcgl = [ar.alloc(64, F32) for _ in range(2)]
    sc_t = [[gcgl[i_][:, 0:32]] + [ar.alloc(32, F32) for _ in range(NSC - 1)] for i_ in range(2)]
    sc_b = [[Buf() for _ in range(NSC)] for _ in range(2)]

    def f32x4():
        return ar.alloc(512, F32)

    def bf16x4():
        return ar.alloc(512, BF16)

    class PB:
        pass
    pbs = []
    for par in range(2):
        o = PB()
        for nm in ("Gd", "d4", "E4", "tmp4", "tmpa", "L0", "L1", "M0", "M1", "P0", "P1", "u4"):
            setattr(o, nm, f32x4())
            setattr(o, nm + "_b", Buf(nm))
        for nm in ("attnT", "kgl", "wT", "vb", "kbg", "P6b"):
            setattr(o, nm, bf16x4())
            setattr(o, nm + "_b", Buf(nm))
        pbs.append(o)
    vnew = [bf16x4() for _ in range(2)]
    vnewb = [Buf(), Buf()]
    tq = [f32x4() for _ in range(2)]
    tqb = [Buf(), Buf()]
    o4 = [f32x4() for _ in range(2)]
    o4b = [Buf(), Buf()]
    osq = f32x4()
    osqb = Buf()
    ss4 = [ar.alloc(4, F32) for _ in range(2)]
    ss4b = [Buf(), Buf()]
    og4 = [f32x4() for _ in range(2)]
    og4b = [Buf(), Buf()]
    ogb4 = [bf16x4() for _ in range(2)]
    ogb4b = [Buf(), Buf()]
    ogT = [bf16x4() for _ in range(2)]
    ogTb = [Buf(), Buf()]
    ident_b = k.ident

    def prep(t, grp, par, ts):
        B = pbs[par]
        s = t % 2
        hv0, hq0 = 4 * grp, 2 * grp
        gc, eg, egla, egl, beg, negbet, dgl = sc_t[ts]
        gcb, egb, eglab, eglb, begb, negbetb, dglb = sc_b[ts]
        gc4, bet4 = gc[:, hv0:hv0 + 4], beta_all[:, t, hv0:hv0 + 4]
        bX, bY, bZ = 3 * par, 3 * par + 1, 3 * par + 2
        psX, psY, psZ = k.psum[bX], k.psum[bY], k.psum[bZ]
        for r in range(2):
            p.op("pe", _mm(psZ[:, r * 128:(r + 1) * 128], qkT[s][:, 16 + hq0 + r, :], ident_b, True, True), [qkTb[s], k.identb], [k.psb[bZ]], lhs=qkTb[s])
        for r in range(4):
            p.op("pe", _mm(psY[:, r * 128:(r + 1) * 128], vT[s][:, hv0 + r, :], ident_b, True, True), [vTb[s], k.identb], [k.psb[bY]], lhs=vTb[s])
        p.op("dve", _tt(_v4(B.Gd), _bo(ident_f, 4), _bc(gc4), ALU.mult), [cfb, gcb], [B.Gd_b])
        p.op("pe", _mm(psX[:, :], ones_f, B.Gd, True, True), [onesb, B.Gd_b], [k.psb[bX]], lhs=B.Gd_b)
        yield
        p.op("dve", _tt(_v4(B.d4), _v4(psX), _bc(gc4), ALU.subtract), [k.psb[bX], gcb], [B.d4_b])
        p.op("dve", _stt(B.d4, B.d4, -1.0, B.d4, ALU.mult, ALU.min), [B.d4_b], [B.d4_b])
        p.op("act", _act(B.E4, B.d4, AF.Exp), [B.d4_b], [B.E4_b])
        for r in range(2):
            p.op("pe", _mm(psX[:, r * 128:(r + 1) * 128], qkT[s][:, 16 + hq0 + r, :], qkT[s][:, 16 + hq0 + r, :], True, True), [qkTb[s]], [k.psb[bX]], lhs=qkTb[s])
        for r in range(2):
            p.op("pe", _mm(psX[:, 256 + r * 128:256 + (r + 1) * 128], qkT[s][:, 16 + hq0 + r, :], qkT[s][:, hq0 + r, :], True, True), [qkTb[s]], [k.psb[bX]], lhs=qkTb[s])
        p.op("dve", _tt(_v4(B.vb), _v4(psY), _bc(bet4), ALU.mult), [k.psb[bY], bab[t]], [B.vb_b])
        p.op("dve", _tt(_v22(B.kbg), _rep2(psZ[:, 0:256]), _bc22(beg[:, hv0:hv0 + 4]), ALU.mult), [k.psb[bZ], begb], [B.kbg_b])
        p.op("dve", _tt(_v22(B.kgl), _rep2(psZ[:, 0:256]), _bc22(egl[:, hv0:hv0 + 4]), ALU.mult), [k.psb[bZ], eglb], [B.kgl_b])
        yield
        p.op("dve", _tt(_v22(B.tmp4), _rep2(psX[:, 0:256]), _v22(B.E4), ALU.mult), [k.psb[bX], B.E4_b], [B.tmp4_b])
        p.op("pool", _tt(_v4(B.tmp4), _v4(B.tmp4), _bc(negbet[:, hv0:hv0 + 4]), ALU.mult), [B.tmp4_b, negbetb], [B.tmp4_b])
        p.op("pool", _tt(_v4(B.L0), _v4(B.tmp4), _bo(sl_f, 4), ALU.mult), [B.tmp4_b, cfb], [B.L0_b])
        p.op("dve", _tt(_v22(B.tmpa), _rep2(psX[:, 256:512]), _v22(B.E4), ALU.mult), [k.psb[bX], B.E4_b], [B.tmpa_b])
        p.op("pool", _tt(_v4(B.attnT), _v4(B.tmpa), _bo(ui_f, 4), ALU.mult), [B.tmpa_b, cfb], [B.attnT_b])
        yield
        for r in range(4):
            p.op("pe", _mm(psY[:, r * 128:(r + 1) * 128], B.L0[:, r * 128:(r + 1) * 128], ident_f, True, True), [B.L0_b, cfb], [k.psb[bY]], lhs=B.L0_b)
        p.op("act", lambda e, o=B.M0, i_=psY: e.copy(out=o, in_=i_), [k.psb[bY]], [B.M0_b])
        p.op("pool", _tt(_v4(B.P0), _v4(B.M0), _bo(ident_f, 4), ALU.add), [B.M0_b, cfb], [B.P0_b])
        yield
        L = [(B.L0, B.L0_b), (B.L1, B.L1_b)]
        M = [(B.M0, B.M0_b), (B.M1, B.M1_b)]
        P = [(B.P0, B.P0_b), (B.P1, B.P1_b)]
        for m in range(6):
            (Lc, Lcb), (Ln, Lnb) = L[m % 2], L[(m + 1) % 2]
            (Mc, Mcb), (Mn, Mnb) = M[m % 2], M[(m + 1) % 2]
            (Pc, Pcb), (Pn, Pnb) = P[m % 2], P[(m + 1) % 2]
            for r in range(4):
                cs_ = slice(r * 128, (r + 1) * 128)
                p.op("pe", _mm(psX[:, cs_], Mc[:, cs_], Lc[:, cs_], True, True), [Mcb, Lcb], [k.psb[bX]], lhs=Mcb)
            if m < 5:
                for r in range(4):
                    cs_ = slice(r * 128, (r + 1) * 128)
                    p.op("pe", _mm(psY[:, cs_], Lc[:, cs_], Mc[:, cs_], True, True), [Mcb, Lcb], [k.psb[bY]], lhs=Lcb)
            p.op("dve", _copy(Ln, psX), [k.psb[bX]], [Lnb])
            if m < 5:
                p.op("act", lambda e, o=Mn, i_=psY: e.copy(out=o, in_=i_), [k.psb[bY]], [Mnb])
            for r in range(4):
                cs_ = slice(r * 128, (r + 1) * 128)
                p.op("pe", _mm(psZ[:, cs_], Ln[:, cs_], Pc[:, cs_], True, True), [Lnb, Pcb], [k.psb[bZ]], lhs=Lnb)
            if m == 5:
                p.op("dve", _tt(B.P6b, psZ, Pc, ALU.add), [k.psb[bZ], Pcb], [B.P6b_b])
            else:
                p.op("dve", _tt(Pn, psZ, Pc, ALU.add), [k.psb[bZ], Pcb], [Pnb])
            yield
        Pf, Pfb = B.P6b, B.P6b_b
        for r in range(4):
            cs_ = slice(r * 128, (r + 1) * 128)
            p.op("pe", _mm(psX[:, cs_], Pf[:, cs_], B.vb[:, cs_], True, True), [Pfb, B.vb_b], [k.psb[bX]], lhs=Pfb)
        for r in range(4):
            cs_ = slice(r * 128, (r + 1) * 128)
            p.op("pe", _mm(psY[:, cs_], B.kbg[:, cs_], Pf[:, cs_], True, True), [Pfb, B.kbg_b], [k.psb[bY]], lhs=B.kbg_b)
        p.op("act", lambda e, o=B.u4, i_=psX: e.copy(out=o, in_=i_), [k.psb[bX]], [B.u4_b])
        p.op("dve", _copy(B.wT, psY), [k.psb[bY]], [B.wT_b])
        yield

    seqn = [0]

    def seq(t, grp, par, ts):
        B = pbs[par]
        s = t % 2
        q2 = seqn[0] % 2
        seqn[0] += 1
        hv0, hq0 = 4 * grp, 2 * grp
        gc, eg, egla, egl, beg, negbet, dgl = sc_t[ts]
        gcb, egb, eglab, eglb, begb, negbetb, dglb = sc_b[ts]
        S4f = S_f[:, hv0 * 128:(hv0 + 4) * 128]
        S4b = S_b[:, hv0 * 128:(hv0 + 4) * 128]
        psW, psQ, psO, psS = k.psum[6], k.psum[7], k.psum[6], k.psum[7]
        bW, bQ, bO, bS = 6, 7, 6, 7
        for r in range(4):
            cs_ = slice(r * 128, (r + 1) * 128)
            p.op("pe", _mm(psW[:, cs_], B.wT[:, cs_], S4b[:, cs_], True, True), [B.wT_b, Sbb[grp]], [k.psb[bW]], lhs=B.wT_b)
        for r in range(4):
            cs_ = slice(r * 128, (r + 1) * 128)
            p.op("pe", _mm(psQ[:, cs_], qkT[s][:, hq0 + r // 2, :], S4b[:, cs_], True, True), [qkTb[s], Sbb[grp]], [k.psb[bQ]], lhs=qkTb[s])
        p.op("dve", _tt(vnew[q2], B.u4, psW, ALU.subtract), [B.u4_b, k.psb[bW]], [vnewb[q2]])
        p.op("dve", _tt(_v4(tq[q2]), _v4(psQ), _bc(eg[:, hv0:hv0 + 4]), ALU.mult), [k.psb[bQ], egb], [tqb[q2]])
        for r in range(4):
            cs_ = slice(r * 128, (r + 1) * 128)
            p.op("pe", _mm(psO[:, cs_], B.attnT[:, cs_], vnew[q2][:, cs_], True, True), [B.attnT_b, vnewb[q2]], [k.psb[bO]], lhs=B.attnT_b)
        for r in range(4):
            cs_ = slice(r * 128, (r + 1) * 128)
            p.op("pe", _mm(psS[:, cs_], B.kgl[:, cs_], vnew[q2][:, cs_], True, True), [B.kgl_b, vnewb[q2]], [k.psb[bS]], lhs=B.kgl_b)
        p.op("dve", _tt(o4[q2], tq[q2], psO, ALU.add), [tqb[q2], k.psb[bO]], [o4b[q2]])
        p.op("pool", _tt(_v4(S4f), _v4(S4f), _bc(egla[:, hv0:hv0 + 4]), ALU.mult), [Sfb[grp], eglab], [Sfb[grp]])
        p.op("dve", _tt(S4f, S4f, psS, ALU.add), [Sfb[grp], k.psb[bS]], [Sfb[grp]])
        p.op("act", lambda e, o=S4b, i_=S4f: e.copy(out=o, in_=i_), [Sfb[grp]], [Sbb[grp]])
        p.op("pool", _tt(osq, o4[q2], o4[q2], ALU.mult), [o4b[q2]], [osqb])
        p.op("dve", lambda e, o=ss4[q2], i_=_v4(osq): e.tensor_reduce(out=o, in_=i_, axis=mybir.AxisListType.X, op=ALU.add), [osqb], [ss4b[q2]])
        p.op("act", _act(ss4[q2], ss4[q2], AF.Sqrt, scale=1.0 / 128.0, bias=k.eps_ap), [ss4b[q2], k.epsb], [ss4b[q2]])
        p.op("dve", lambda e, o=ss4[q2]: e.reciprocal(out=o, in_=o), [ss4b[q2]], [ss4b[q2]])
        p.op("dve", _tt(_v4(og4[q2]), _v4(o4[q2]), _bc(ss4[q2]), ALU.mult), [o4b[q2], ss4b[q2]], [og4b[q2]])
        p.op("pool", _tt(_v4(og4[q2]), _v4(og4[q2]), _bo(og_bc, 4), ALU.mult), [og4b[q2], cfb], [og4b[q2]])
        p.op("pool", _tt(ogb4[q2], og4[q2], zt[s][:, hv0 * 128:(hv0 + 4) * 128], ALU.mult), [og4b[q2], ztb[s]], [ogb4b[q2]])
        for r in range(4):
            cs_ = slice(r * 128, (r + 1) * 128)
            p.op("pe", _mm(psW[:, cs_], ogb4[q2][:, cs_], ident_b, True, True), [ogb4b[q2], k.identb], [k.psb[bW]], lhs=ogb4b[q2])
        p.op("act", lambda e, o=ogT[q2], i_=psW: e.copy(out=o, in_=i_), [k.psb[bW]], [ogTb[q2]])
        p.op("sp", _dma(oT_s[hv0:hv0 + 4, :, t * 128:(t + 1) * 128].rearrange("h p c -> p h c"), _v4(ogT[q2])),
             [ogTb[q2]], [oTb[hv0 + r][t] for r in range(4)], dsem=f"ogT{q2}")

    gctr = 0
    import os
    ntl = int(os.environ.get("DN_TILES", NT))
    maxstep = int(os.environ.get("DN_MAXSTEP", 99))
    def tile_loads(t_):
        s_ = t_ % 2
        for hh in range(4):
            hs = slice(hh * 8, (hh + 1) * 8)
            p.op("sp", _dma(qkT[s_][:, hs, :], qk_s[hs, :, t_ * 128:(t_ + 1) * 128].rearrange("h p c -> p h c")), qksb[hs], [qkTb[s_]], dsem=f"qkT{s_}")
            p.op("sp", _dma(vT[s_][:, hs, :], v_s[hs, :, t_ * 128:(t_ + 1) * 128].rearrange("h p c -> p h c")), vsb[hs], [vTb[s_]], dsem=f"vT{s_}")
        p.op("sp", _dma(zt[s_], z_s[t_ * 128:(t_ + 1) * 128, :]), [zsb[t_]], [ztb[s_]], dsem=f"zt{s_}")

    for t in range(ntl):
        s = t % 2
        ts = t % 2
        tile_loads(t)
        gc, eg, egla, egl, beg, negbet, dgl = sc_t[ts]
        gcb, egb, eglab, eglb, begb, negbetb, dglb = sc_b[ts]
        psG = k.psum[7]
        p.op("pe", _mm(psG[:, 0:32], ui_f, g_all[:, t, :], True, True), [cfb, gab[t]], [k.psb[7]], lhs=cfb)
        p.op("pe", _mm(psG[:, 32:64], ones_f, g_all[:, t, :], True, True), [onesb, gab[t]], [k.psb[7]], lhs=onesb)
        p.op("dve", _copy(gcgl[ts], psG[:, 0:64]), [k.psb[7]], [gcb])
        p.op("act", _act(eg, gcgl[ts][:, 0:32], AF.Exp), [gcb], [egb])
        p.op("act", _act(egla, gcgl[ts][:, 32:64], AF.Exp), [gcb], [eglab])
        p.op("dve", _tt(dgl, gcgl[ts][:, 32:64], gcgl[ts][:, 0:32], ALU.subtract), [gcb], [dglb])
        p.op("act", _act(egl, dgl, AF.Exp), [dglb], [eglb])
        p.op("dve", _tt(beg, beta_all[:, t, :], eg, ALU.mult), [bab[t], egb], [begb])
        p.op("dve", _ts(negbet, beta_all[:, t, :], -1.0, None, ALU.mult), [bab[t]], [negbetb])
        active = []
        grp = 0
        while grp < 8 or active:
            while grp < 8 and len(active) < 2:
                active.append((prep(t, grp, gctr % 2, ts), grp, gctr % 2, 0))
                gctr += 1
                grp += 1
            nxt = []
            for (gen, gg, par, nst) in active:
                if nst >= maxstep:
                    continue
                try:
                    next(gen)
                    nxt.append((gen, gg, par, nst + 1))
                except StopIteration:
                    if maxstep >= 99:
                        seq(t, gg, par, ts)
            active = nxt
    p.barrier()
    ar.pop()
    if "O" not in phases:
        ar.pop()
        return
    for half in range(2):
        ar.push()
        oT = ar.alloc(16 * NTP, BF16).rearrange("p (a b) -> p a b", a=16)
        oTl = [Buf(f"oTl{c}") for c in range(16)]
        for c in range(16):
            hv = half * 16 + c
            p.op("sp", _dma(oT[:, c, :], oT_s[hv]), oTb[hv], [oTl[c]], dsem=f"oTl{c % 4}")
        out_proj(k, oT, [[oTl[c]] * NT for c in range(16)], 16, w_out[half * 2048:(half + 1) * 2048, :], 1.0)
        ar.pop()
    ar.pop()


def init_stage(k, x, meta):
    p, ar = k.p, k.ar
    ar.push()
    z = ar.alloc(D, F32)
    zb = Buf("z")
    p.op("dve", _memset(z, 0.0), [], [zb])
    p.op("sp", _dma(k.h[0:112, :], z[0:112, :]), [zb], [k.hb[0]], dsem="h0")
    p.op("sp", _dma(k.h[112:128, :], meta[:, :]), [], [k.hb[0]], dsem="h0")
    for t in range(1, NT):
        p.op("sp", _dma(k.h[t * 128:(t + 1) * 128, :], x[(t - 1) * 128:t * 128, :]), [], [k.hb[t]], dsem=f"h{t}")
    p.op("dve", _memset(k.eps_ap, EPS), [], [k.epsb])
    p.op("pool", _dma(k.ident, k.ident_dram[:, :]), [], [k.identb], dsem="const")
    p.barrier()
    ar.pop()


def final_stage(k, gain_row, out):
    p, ar = k.p, k.ar
    ar.push()
    gain_bc = ar.alloc(D, F32)
    gb = Buf("gain")
    p.op("sp", _dma(gain_bc, gain_row.partition_broadcast(128)), [], [gb], dsem="gain")
    hin = [ar.alloc(D, F32) for _ in range(2)]
    hinb = [Buf(), Buf()]
    ho = [ar.alloc(D, F32) for _ in range(2)]
    hob = [Buf(), Buf()]
    junk = ar.alloc(D, BF16)
    junkb = Buf()
    ss = [ar.alloc(1, F32) for _ in range(2)]
    ssb = [Buf(), Buf()]
    rs = [ar.alloc(1, F32) for _ in range(2)]
    rsb = [Buf(), Buf()]
    ob = Buf("out")
    for t in range(1, NT):
        s = t % 2
        p.op("sp", _dma(hin[s], k.h[t * 128:(t + 1) * 128, :]), [k.hb[t]], [hinb[s]], dsem=f"hin{s}")
        p.op("act", _act(junk, hin[s], AF.Square, accum_out=ss[s]), [hinb[s]], [junkb, ssb[s]])
        p.op("act", _act(ss[s], ss[s], AF.Sqrt, scale=1.0 / D, bias=k.eps_ap), [ssb[s], k.epsb], [ssb[s]])
        p.op("dve", lambda e, o=rs[s], i=ss[s]: e.reciprocal(out=o, in_=i), [ssb[s]], [rsb[s]])
        p.op("dve", _stt(ho[s], hin[s], rs[s][:, 0:1], gain_bc, ALU.mult, ALU.mult), [hinb[s], rsb[s], gb], [hob[s]])
        p.op("sp", _dma(out[(t - 1) * 128:t * 128, :], ho[s]), [hob[s]], [ob], dsem=f"out{s}")
    p.barrier()
    ar.pop()


INPUT_NAMES = ["x", "meta_tokens", "ffn_pre_norm", "ffn_pre_w_gu", "ffn_pre_w_down", "mix_norm",
               "ffn_post_norm", "ffn_post_w_gu", "ffn_post_w_down", "dn_w_in", "dn_conv_w", "dn_a_log",
               "dn_dt_bias", "dn_out_norm", "dn_w_out", "swa_w_qkv", "swa_sinks", "swa_w_out", "final_norm"]

PER_CORE_SHAPES = {
    "x": [SEQ, D], "meta_tokens": [NMETA, D], "ffn_pre_norm": [DEPTH, D], "ffn_pre_w_gu": [DEPTH * D, 2 * DFF],
    "ffn_pre_w_down": [DEPTH * DFF, D], "mix_norm": [DEPTH, D], "ffn_post_norm": [DEPTH, D],
    "ffn_post_w_gu": [DEPTH * D, 2 * DFF], "ffn_post_w_down": [DEPTH * DFF, D],
    "dn_w_in": [2 * D, 12352], "dn_conv_w": [2 * 128, 256], "dn_a_log": [2, 32], "dn_dt_bias": [2, 32],
    "dn_out_norm": [2, 128], "dn_w_out": [2 * 4096, D], "swa_w_qkv": [2 * D, 2560], "swa_sinks": [4, 16],
    "swa_w_out": [2 * D, D], "final_norm": [1, D],
}


def build_program(stages, dbg_h=False, debug=False, only_inputs=None):
    nc = bass.Bass("TRN2", target_bir_lowering=False)
    ins = {n: nc.dram_tensor(n, PER_CORE_SHAPES[n], F32, kind="ExternalInput").ap() for n in INPUT_NAMES
           if only_inputs is None or n in only_inputs}
    ident_dram = nc.dram_tensor("c_ident", [128, 128], F32, kind="ExternalInput").ap()
    cdr = {n: nc.dram_tensor(n, list(a.shape), F32, kind="ExternalInput").ap() for n, a in make_consts().items() if n != "c_ident"}
    out = nc.dram_tensor("out", [SEQ, D], F32, kind="ExternalOutput").ap()
    if dbg_h:
        h = nc.dram_tensor("h_dbg", [NTP, D], F32, kind="ExternalOutput").ap()
    else:
        h = nc.dram_tensor("h_scratch", [NTP, D], F32, kind="Internal").ap()
    import contextlib
    with contextlib.ExitStack() as st:
        big = st.enter_context(nc.sbuf_tensor("big", [128, SB_BYTES // 2], BF16))
        psum = [st.enter_context(nc.psum_tensor(f"ps{i}", [128, 512], F32)) for i in range(8)]
        k = K()
        k.nc = nc
        k.debug = debug
        import os
        k.dn_phases = os.environ.get("DN_PHASES", "PRO")
        k.p = Prog(nc)
        k.ar = Arena(big, SB_BYTES)
        k.psum = [t[:] for t in psum]
        k.psb = [Buf(f"ps{i}") for i in range(8)]
        k.h = h
        k.hb = [Buf(f"h{t}") for t in range(NT)]
        k.ident = k.ar.alloc(128, BF16)
        k.identb = Buf("ident")
        k.ident_dram = ident_dram
        k.c_cos, k.c_sin = cdr["c_cos"], cdr["c_sin"]
        k.c_ui, k.c_sl = cdr["c_ui"], cdr["c_sl"]
        k.c_mask3, k.c_mask0, k.c_onesp, k.c_onesmp = cdr["c_mask3"], cdr["c_mask0"], cdr["c_onesp"], cdr["c_onesmp"]
        k.eps_ap = k.ar.alloc(1, F32)
        k.epsb = Buf("eps")
        k.ins = ins
        for sname in stages:
            if sname == "init":
                init_stage(k, ins["x"], ins["meta_tokens"])
            elif sname.startswith("pre") or sname.startswith("post"):
                which = "pre" if sname.startswith("pre") else "post"
                i = int(sname[len(which):])
                ffn_stage(k, ins[f"ffn_{which}_norm"][i, :],
                          ins[f"ffn_{which}_w_gu"][i * D:(i + 1) * D, :],
                          ins[f"ffn_{which}_w_down"][i * DFF:(i + 1) * DFF, :])
            elif sname.startswith("mix"):
                i = int(sname[3:])
                j = i // 2
                if i % 2 == 1:
                    swa_stage(k, ins["mix_norm"][i, :], ins["swa_w_qkv"][j * D:(j + 1) * D, :],
                              ins["swa_sinks"][2 * j:2 * j + 2, :], ins["swa_w_out"][j * D:(j + 1) * D, :])
                else:
                    dn_stage(k, i, j)
            elif sname == "final":
                final_stage(k, ins["final_norm"][0, :], out)
            else:
                raise ValueError(sname)
        k.p.emit()
    return nc, k.p.nops


def make_consts():
    c = {}
    c["c_ident"] = np.eye(128, dtype=np.float32)
    pidx = np.arange(128)
    col = np.arange(NTP)
    pos = np.maximum(col - 112, 0).astype(np.float32)
    inv = (10000.0 ** (-np.arange(0, 64, 2, dtype=np.float32) / 64.0)).astype(np.float32)
    f = (pidx % 64) % 32
    ang = pos[None, :] * inv[f][:, None]
    sign = np.where((pidx % 64) < 32, -1.0, 1.0).astype(np.float32)
    c["c_cos"] = np.cos(ang).astype(np.float32)
    c["c_sin"] = (np.sin(ang) * sign[:, None]).astype(np.float32)
    jj = np.arange(128)[:, None]
    ii = np.arange(128)[None, :]
    cur = (jj <= ii).astype(np.float32)
    prev = (jj > ii).astype(np.float32)
    c["c_mask3"] = np.concatenate([cur, np.ones((128, 128), np.float32), prev], axis=1)
    c["c_mask0"] = ((jj <= ii) & (jj >= 112)).astype(np.float32)
    op = np.zeros((128, 2, 128), np.float32)
    op[:, 0, 0:64] = 1.0
    op[:, 1, 64:128] = 1.0
    c["c_onesp"] = op.reshape(128, 256)
    om = op.copy()
    om[:112] = 0.0
    c["c_onesmp"] = om.reshape(128, 256)
    pp = np.arange(128)[:, None]
    ff = np.arange(128)[None, :]
    c["c_ui"] = (ff >= pp).astype(np.float32)
    c["c_sl"] = (ff < pp).astype(np.float32)
    return c


def make_in_maps(inputs, n_cores=8, only_inputs=None):
    shared = dict(make_consts())
    for n in INPUT_NAMES:
        if n == "x" or (only_inputs is not None and n not in only_inputs):
            continue
        a = np.ascontiguousarray(np.asarray(inputs[n], dtype=np.float32))
        if n == "dn_conv_w":
            a = np.ascontiguousarray(a.reshape(a.shape[0], 4, 64, 128).transpose(0, 3, 2, 1))
        if n == "swa_sinks":
            a = np.ascontiguousarray(a.reshape(a.shape[0], 16, 2).transpose(0, 2, 1))
        shared[n] = a.reshape(PER_CORE_SHAPES[n])
    x = np.asarray(inputs["x"], dtype=np.float32)
    maps = []
    for c in range(n_cores):
        m = dict(shared)
        m["x"] = np.ascontiguousarray(x[c])
        maps.append(m)
    return maps


ALL_STAGES = ["init"] + [s for i in range(DEPTH) for s in (f"pre{i}", f"mix{i}", f"post{i}")] + ["final"]


def kernel(**inputs):
    nc, _ = build_program(ALL_STAGES)
    maps = make_in_maps(inputs)
    res = run_bass_kernel_spmd(nc, maps, core_ids=list(range(8)))
    return np.stack([r["out"] for r in res.results], axis=0).astype(np.float32)
```

```python
import numpy as np
import concourse.bass as bass
import concourse.mybir as mybir
from concourse.bass_utils import run_bass_kernel_spmd

F32 = mybir.dt.float32
BF16 = mybir.dt.bfloat16
AF = mybir.ActivationFunctionType
ALU = mybir.AluOpType

D = 2048
SEQ = 2048
NMETA = 16
NT = 17
NTP = NT * 128
DFF = 5504
NFC = DFF // 128
KC = D // 128
DEPTH = 4
EPS = 1e-6
SB_BYTES = 212800
TG = [(0, 512), (512, 512), (1024, 512), (1536, 512), (2048, 128)]

ENGS = ("pe", "act", "dve", "pool", "sp")


class Buf:
    __slots__ = ("name", "w", "r", "rd")

    def __init__(self, name=""):
        self.name = name
        self.w = None
        self.r = {}
        self.rd = []


class Op:
    __slots__ = ("eng", "fn", "deps", "is_dma", "dsem", "dval", "signal", "sigidx", "attach")


class Prog:
    def __init__(self, nc):
        self.nc = nc
        self.ops = {e: [] for e in ENGS}
        self.dsems = {}
        self.nops = 0

    def dma_sem(self, name):
        if name not in self.dsems:
            self.dsems[name] = [None, 0]
        return name

    def op(self, eng, fn, reads=(), writes=(), dsem=None, lhs=None):
        o = Op()
        o.attach = None
        if lhs is not None and lhs.w is not None and (lhs.w.is_dma or lhs.w.eng != eng):
            o.attach = lhs.w
        o.eng = eng
        o.fn = fn
        o.is_dma = dsem is not None
        o.signal = False
        o.sigidx = 0
        deps = set()
        for b in reads:
            if b.w is not None:
                deps.add(b.w)
        for b in writes:
            if b.w is not None:
                deps.add(b.w)
            for r in b.r.values():
                deps.add(r)
            for r in b.rd:
                deps.add(r)
        for b in reads:
            if o.is_dma:
                b.rd.append(o)
            else:
                b.r[eng] = o
        for b in writes:
            b.w = o
            b.r = {}
            b.rd = []
        o.deps = [d for d in deps if d.is_dma or not (d.eng == eng and eng == "pe")]
        for d in o.deps:
            if not d.is_dma:
                d.signal = True
        if o.is_dma:
            s = self.dsems.setdefault(dsem, [None, 0])
            s[1] += 16
            o.dsem = dsem
            o.dval = s[1]
        self.ops[eng].append(o)
        self.nops += 1
        return o

    def barrier(self):
        lasts = []
        for e in ENGS:
            last = None
            for o in reversed(self.ops[e]):
                if not o.is_dma and o.fn is not None:
                    last = o
                    break
            if last is not None:
                last.signal = True
                lasts.append(last)
        dlast = {}
        for e in ENGS:
            for o in self.ops[e]:
                if o.is_dma:
                    dlast[o.dsem] = o
        for e in ENGS:
            o = Op()
            o.eng = e
            o.fn = None
            o.is_dma = False
            o.signal = False
            o.sigidx = 0
            o.deps = [l for l in lasts if l.eng != e] + list(dlast.values())
            o.attach = None
            self.ops[e].append(o)

    def emit(self):
        nc = self.nc
        import contextlib
        with contextlib.ExitStack() as st:
            esem = {e: st.enter_context(nc.semaphore("s_" + e)) for e in ENGS}
            for name, s in self.dsems.items():
                s[0] = st.enter_context(nc.semaphore("d_" + name))
            for e in ENGS:
                c = 0
                for o in self.ops[e]:
                    if o.is_dma or o.fn is None:
                        continue
                    if o.signal:
                        c += 1
                        o.sigidx = c
            block = st.enter_context(nc.Block())
            dsems = self.dsems

            def replay(e, engobj):
                waited = {}

                def semval(d):
                    if d.is_dma:
                        return ("d", d.dsem), dsems[d.dsem][0], d.dval
                    return ("e", d.eng), esem[d.eng], d.sigidx

                for o in self.ops[e]:
                    att = o.attach
                    need = {}
                    for d in o.deps:
                        if d is att:
                            continue
                        key, sem, val = semval(d)
                        if waited.get(key, 0) >= val:
                            continue
                        if key not in need or need[key][1] < val:
                            need[key] = (sem, val)
                    pend = []
                    for key, (sem, val) in need.items():
                        waited[key] = val
                        pend.append((sem, val))
                    if att is None and pend and o.fn is not None and not o.is_dma:
                        last = pend.pop()
                    else:
                        last = None
                    for sem, val in pend:
                        engobj.wait_ge(sem, val)
                    if o.fn is None:
                        continue
                    ins = o.fn(engobj)
                    if att is not None:
                        key, sem, val = semval(att)
                        if waited.get(key, 0) < val:
                            waited[key] = val
                        ins._wait_ge(sem, val)
                    elif last is not None:
                        ins._wait_ge(last[0], last[1])
                    if o.is_dma:
                        ins.then_inc(dsems[o.dsem][0], 16)
                    elif o.signal:
                        ins.then_inc(esem[e], 1)

            @block.tensor
            def _(t):
                replay("pe", t)

            @block.scalar
            def _(t):
                replay("act", t)

            @block.vector
            def _(t):
                replay("dve", t)

            @block.gpsimd
            def _(t):
                replay("pool", t)

            @block.sync
            def _(t):
                replay("sp", t)


class Arena:
    def __init__(self, big, nbytes):
        self.big = big
        self.nbytes = nbytes
        self.top = 0
        self.marks = []

    def push(self):
        self.marks.append(self.top)

    def pop(self):
        self.top = self.marks.pop()

    def alloc(self, free_elems, dtype):
        esz = 4 if dtype == F32 else 2
        nb = free_elems * esz
        nb = (nb + 63) // 64 * 64
        off = self.top
        self.top += nb
        assert self.top <= self.nbytes, f"SBUF arena overflow {self.top} > {self.nbytes}"
        v = self.big[:, off // 2:(off + free_elems * esz) // 2]
        if dtype == F32:
            v = v.bitcast(F32)
        return v


class K:
    pass


def _mm(out, lhsT, rhs, start, stop):
    return lambda e: e.matmul(out, lhsT, rhs, start=start, stop=stop)


def _tr(out, in_, ident):
    return lambda e: e.transpose(out, in_, ident)


def _act(out, in_, func, **kw):
    return lambda e: e.activation(out=out, in_=in_, func=func, **kw)


def _tt(out, in0, in1, op):
    return lambda e: e.tensor_tensor(out=out, in0=in0, in1=in1, op=op)


def _ts(out, in0, s1, s2, op0, op1=None, **kw):
    if op1 is None:
        return lambda e: e.tensor_scalar(out=out, in0=in0, scalar1=s1, scalar2=s2, op0=op0, **kw)
    return lambda e: e.tensor_scalar(out=out, in0=in0, scalar1=s1, scalar2=s2, op0=op0, op1=op1, **kw)


def _stt(out, in0, scalar, in1, op0, op1, **kw):
    return lambda e: e.scalar_tensor_tensor(out=out, in0=in0, scalar=scalar, in1=in1, op0=op0, op1=op1, **kw)


def _copy(out, in_):
    return lambda e: e.tensor_copy(out=out, in_=in_)


def _memset(ap, v):
    return lambda e: e.memset(ap, v)


def _dma(out, in_):
    return lambda e: e.dma_start(out=out, in_=in_)


def dump(k, name, ap, reads):
    if not getattr(k, "debug", False):
        return
    shp = list(ap.shape)
    dt = k.nc.dram_tensor("dbg_" + name, shp, ap.dtype, kind="ExternalOutput").ap()
    k.p.op("sp", _dma(dt, ap), reads, [Buf()], dsem="dbg_" + name)


def norm_pass(k, gain_row, xnT, xnT_bufs, skip_tile0_pad=False):
    p, ar = k.p, k.ar
    ar.push()
    gain_bc = ar.alloc(D, F32)
    gb = Buf("gain")
    p.op("sp", _dma(gain_bc, gain_row.partition_broadcast(128)), [], [gb], dsem="gain")
    NH = 4
    hin = [ar.alloc(D, F32) for _ in range(NH)]
    hinb = [Buf(f"hin{i_}") for i_ in range(NH)]
    junk = ar.alloc(D, BF16)
    junkb = Buf("junk")
    xn = [ar.alloc(D, BF16) for _ in range(2)]
    xnb = [Buf("xn0"), Buf("xn1")]
    ss = [ar.alloc(1, F32) for _ in range(2)]
    ssb = [Buf(), Buf()]
    rs = [ar.alloc(1, F32) for _ in range(2)]
    rsb = [Buf(), Buf()]
    for t in range(NT):
        s = t % 2
        sh = t % NH
        p.op("sp", _dma(hin[sh], k.h[t * 128:(t + 1) * 128, :]), [k.hb[t]], [hinb[sh]], dsem=f"hin{sh}")
        p.op("act", _act(junk, hin[sh], AF.Square, accum_out=ss[s]), [hinb[sh]], [junkb, ssb[s]])
        p.op("act", _act(ss[s], ss[s], AF.Sqrt, scale=1.0 / D, bias=k.eps_ap), [ssb[s], k.epsb], [ssb[s]])
        p.op("dve", lambda e, o=rs[s], i=ss[s]: e.reciprocal(out=o, in_=i), [ssb[s]], [rsb[s]])
        p.op("dve", _stt(xn[s], hin[sh], rs[s][:, 0:1], gain_bc, ALU.mult, ALU.mult),
             [hinb[sh], rsb[s], gb], [xnb[s]])
        for q4 in range(4):
            pst = k.psum[4 + q4]
            pb = k.psb[4 + q4]
            for j in range(4):
                kc = q4 * 4 + j
                p.op("pe", _mm(pst[:, j * 128:(j + 1) * 128], xn[s][:, kc * 128:(kc + 1) * 128], k.ident, True, True),
                     [xnb[s], k.identb], [pb], lhs=xnb[s])
            dst = xnT[:, q4 * 4:(q4 + 1) * 4, t * 128:(t + 1) * 128]
            src = pst.rearrange("p (a b) -> p a b", a=4)
            if q4 % 2 == 0:
                p.op("act", lambda e, dst=dst, src=src: e.copy(out=dst, in_=src), [pb], [xnT_bufs[t][q4]])
            else:
                p.op("dve", _copy(dst, src), [pb], [xnT_bufs[t][q4]])
    p.barrier()
    ar.pop()


def ffn_stage(k, gain_row, w_gu, w_down, scale=0.5):
    p, ar = k.p, k.ar
    ar.push()
    xnT = ar.alloc(KC * NTP, BF16).rearrange("p (a b) -> p a b", a=KC)
    xnTb = [[Buf(f"xnT{t}_{q}") for q in range(4)] for t in range(NT)]
    norm_pass(k, gain_row, xnT, xnTb)

    GC = 8
    ng = (NFC + GC - 1) // GC
    groups = []
    c = 0
    for gi_ in range(ng):
        n = NFC // ng + (1 if gi_ < NFC % ng else 0)
        groups.append((c, n))
        c += n
    hT = ar.alloc(GC * NTP, BF16).rearrange("p (a b) -> p a b", a=GC)
    hTb = [[Buf(f"hT{ci}_{t}") for t in range(NT)] for ci in range(GC)]
    wd = [ar.alloc(D, BF16) for _ in range(GC)]
    wdb = [Buf(f"wd{ci}") for ci in range(GC)]
    NWS = 2
    wg = [ar.alloc(KC * 256, BF16).rearrange("p (a b) -> p a b", a=KC) for _ in range(NWS)]
    wu = [ar.alloc(KC * 256, BF16).rearrange("p (a b) -> p a b", a=KC) for _ in range(NWS)]
    wgb = [Buf(f"wg{i}") for i in range(NWS)]
    wub = [Buf(f"wu{i}") for i in range(NWS)]
    sg = [ar.alloc(512, F32) for _ in range(2)]
    sgb = [Buf("sg0"), Buf("sg1")]
    hio = [ar.alloc(D, F32) for _ in range(4)]
    wgu_v = w_gu.rearrange("(kc p) n -> p kc n", p=128)

    hioq = [[Buf(f"hio{s}_{j}") for j in range(4)] for s in range(4)]
    pair_idx = 0
    tgi = 0
    tile_ctr = 0
    n_tiles_total = len(groups) * NT
    issued = [0]

    def ensure_loads(upto):
        while issued[0] <= min(upto, n_tiles_total - 1):
            n_ = issued[0]
            t_, s_ = n_ % NT, n_ % 4
            p.op("sp", _dma(hio[s_], k.h[t_ * 128:(t_ + 1) * 128, :]), [k.hb[t_]], hioq[s_], dsem=f"hio{s_}")
            issued[0] += 1

    for (c0, gn) in groups:
        ci = 0
        first = True
        while ci < gn:
            npair = min(2, gn - ci)
            s = pair_idx % NWS
            pair_idx += 1
            cc = c0 + ci
            p.op("pool", _dma(wg[s][:, :, 0:npair * 128], wgu_v[:, :, cc * 128:(cc + npair) * 128]), [], [wgb[s]], dsem=f"wg{s}")
            p.op("pool", _dma(wu[s][:, :, 0:npair * 128], wgu_v[:, :, DFF + cc * 128:DFF + (cc + npair) * 128]), [], [wub[s]], dsem=f"wu{s}")
            if first:
                first = False
                for cj in range(gn):
                    p.op("pool", _dma(wd[cj], w_down[(c0 + cj) * 128:(c0 + cj + 1) * 128, :]), [], [wdb[cj]], dsem=f"wd{cj}")
            for q in range(npair):
                for (t0, tn) in TG:
                    b2 = tgi % 2
                    tgi += 1
                    psG, psU = k.psum[b2 * 2], k.psum[b2 * 2 + 1]
                    pGb, pUb = k.psb[b2 * 2], k.psb[b2 * 2 + 1]
                    xr = [b for t in range(t0 // 128, (t0 + tn) // 128) for b in xnTb[t]]
                    for kc in range(KC):
                        p.op("pe", _mm(psG[:, 0:tn], wg[s][:, kc, q * 128:(q + 1) * 128], xnT[:, kc, t0:t0 + tn], kc == 0, kc == KC - 1),
                             [wgb[s]] + xr, [pGb], lhs=wgb[s])
                    for kc in range(KC):
                        p.op("pe", _mm(psU[:, 0:tn], wu[s][:, kc, q * 128:(q + 1) * 128], xnT[:, kc, t0:t0 + tn], kc == 0, kc == KC - 1),
                             [wub[s]] + xr, [pUb], lhs=wub[s])
                    p.op("act", _act(sg[b2][:, 0:tn], psG[:, 0:tn], AF.Silu), [pGb], [sgb[b2]])
                    p.op("dve", _tt(hT[:, ci + q, t0:t0 + tn], sg[b2][:, 0:tn], psU[:, 0:tn], ALU.mult),
                         [sgb[b2], pUb], [hTb[ci + q][t] for t in range(t0 // 128, (t0 + tn) // 128)])
            ci += npair
        for t in range(NT):
            s = tile_ctr % 4
            ensure_loads(tile_ctr + 2)
            tile_ctr += 1
            pb0 = 4 if tile_ctr % 2 == 1 else 0
            for ci in range(gn):
                for j in range(4):
                    wr = [k.psb[pb0 + jj] for jj in range(4)] if (ci == 0 and j == 0) else [k.psb[pb0 + j]]
                    p.op("pe", _mm(k.psum[pb0 + j][:, :], hT[:, ci, t * 128:(t + 1) * 128], wd[ci][:, j * 512:(j + 1) * 512], ci == 0, ci == gn - 1),
                         [hTb[ci][t], wdb[ci]], wr, lhs=hTb[ci][t])
            for j in range(4):
                p.op("dve", _stt(hio[s][:, j * 512:(j + 1) * 512], k.psum[pb0 + j][:, :], scale, hio[s][:, j * 512:(j + 1) * 512], ALU.mult, ALU.add),
                     [k.psb[pb0 + j], hioq[s][j]], [hioq[s][j]])
            p.op("sp", _dma(k.h[t * 128:(t + 1) * 128, :], hio[s]), hioq[s], [k.hb[t]], dsem=f"h{t}")
    p.barrier()
    ar.pop()


def out_proj(k, actT, act_bufs, n, w_rows, scale, dsem_prefix="op"):
    p, ar = k.p, k.ar
    ar.push()
    w = ar.alloc(n * D, BF16).rearrange("p (a b) -> p a b", a=n)
    wb = [Buf(f"opw{ci}") for ci in range(n)]
    for ci in range(n):
        p.op("pool", _dma(w[:, ci, :], w_rows[ci * 128:(ci + 1) * 128, :]), [], [wb[ci]], dsem=f"{dsem_prefix}w{ci % 4}")
    hio = [ar.alloc(D, F32) for _ in range(4)]
    hioq = [[Buf(f"ophio{s}_{j}") for j in range(4)] for s in range(4)]
    issued = [0]

    def ensure_loads(upto):
        while issued[0] <= min(upto, NT - 1):
            t_ = issued[0]
            p.op("sp", _dma(hio[t_ % 4], k.h[t_ * 128:(t_ + 1) * 128, :]), [k.hb[t_]], hioq[t_ % 4], dsem=f"hio{t_ % 4}")
            issued[0] += 1

    for t in range(NT):
        s = t % 4
        ensure_loads(t + 2)
        pb0 = 4 if t % 2 == 1 else 0
        for ci in range(n):
            for j in range(4):
                wr = [k.psb[pb0 + jj] for jj in range(4)] if (ci == 0 and j == 0) else [k.psb[pb0 + j]]
                p.op("pe", _mm(k.psum[pb0 + j][:, :], actT[:, ci, t * 128:(t + 1) * 128], w[:, ci, j * 512:(j + 1) * 512], ci == 0, ci == n - 1),
                     [act_bufs[ci][t], wb[ci]], wr, lhs=act_bufs[ci][t])
        for j in range(4):
            p.op("dve", _stt(hio[s][:, j * 512:(j + 1) * 512], k.psum[pb0 + j][:, :], scale, hio[s][:, j * 512:(j + 1) * 512], ALU.mult, ALU.add),
                 [k.psb[pb0 + j], hioq[s][j]], [hioq[s][j]])
        if t == 0:
            p.op("sp", _dma(k.h[112:128, :], hio[s][112:128, :]), hioq[s], [k.hb[t]], dsem=f"h{t}")
        else:
            p.op("sp", _dma(k.h[t * 128:(t + 1) * 128, :], hio[s]), hioq[s], [k.hb[t]], dsem=f"h{t}")
    p.barrier()
    ar.pop()


def swa_stage(k, gain_row, w_qkv, sinks2, w_out):
    p, ar = k.p, k.ar
    ar.push()
    xnT = ar.alloc(KC * NTP, BF16).rearrange("p (a b) -> p a b", a=KC)
    xnTb = [[Buf(f"xnT{t}_{q}") for q in range(4)] for t in range(NT)]
    norm_pass(k, gain_row, xnT, xnTb)
    qT = ar.alloc(16 * NTP, BF16).rearrange("p (a b) -> p a b", a=16)
    kT = ar.alloc(4 * NTP, BF16).rearrange("p (a b) -> p a b", a=4)
    qkb = [Buf(f"qk{c}") for c in range(20)]
    ar.push()
    cosT = ar.alloc(NTP, F32)
    sinT = ar.alloc(NTP, F32)
    tabb = Buf("tab")
    p.op("sp", _dma(cosT, k.c_cos[:, :]), [], [tabb], dsem="tab")
    p.op("sp", _dma(sinT, k.c_sin[:, :]), [], [tabb], dsem="tab")
    wA = [ar.alloc(KC * 128, BF16).rearrange("p (a b) -> p a b", a=KC) for _ in range(2)]
    wP = [ar.alloc(KC * 128, BF16).rearrange("p (a b) -> p a b", a=KC) for _ in range(2)]
    wAb = [Buf("wA0"), Buf("wA1")]
    wPb = [Buf("wP0"), Buf("wP1")]
    t1 = [ar.alloc(512, F32) for _ in range(2)]
    t2 = [ar.alloc(512, F32) for _ in range(2)]
    t1b = [Buf(), Buf()]
    t2b = [Buf(), Buf()]
    wq_v = w_qkv.rearrange("(kc p) n -> p kc n", p=128)
    tgi = 0
    for c in range(20):
        s = c % 2
        if c < 16:
            base = c * 128
            piecesA = [(0, base, 128)]
            piecesP = [(0, base + 32, 32), (32, base, 32), (64, base + 96, 32), (96, base + 64, 32)]
            dst = qT[:, c, :]
        else:
            base = 2048 + (c - 16) * 64
            piecesA = [(0, base, 64), (64, base, 64)]
            piecesP = [(0, base + 32, 32), (32, base, 32), (64, base + 32, 32), (96, base, 32)]
            dst = kT[:, c - 16, :]
        for (d0, s0, n) in piecesA:
            p.op("pool", _dma(wA[s][:, :, d0:d0 + n], wq_v[:, :, s0:s0 + n]), [], [wAb[s]], dsem=f"wA{s}")
        srcv = wA[s].rearrange("p k (h f c) -> p k h f c", h=2, f=2)
        dstv = wP[s].rearrange("p k (h f c) -> p k h f c", h=2, f=2)
        for f_ in range(2):
            p.op("act", lambda e, o=dstv[:, :, :, f_, :], i_=srcv[:, :, :, 1 - f_, :]: e.copy(out=o, in_=i_), [wAb[s]], [wPb[s]])
        for (t0, tn) in TG:
            b2 = tgi % 2
            tgi += 1
            psA, psP = k.psum[b2 * 2], k.psum[b2 * 2 + 1]
            pAb, pPb = k.psb[b2 * 2], k.psb[b2 * 2 + 1]
            xr = [b for t in range(t0 // 128, (t0 + tn) // 128) for b in xnTb[t]]
            for kc in range(KC):
                p.op("pe", _mm(psA[:, 0:tn], wA[s][:, kc, :], xnT[:, kc, t0:t0 + tn], kc == 0, kc == KC - 1), [wAb[s]] + xr, [pAb], lhs=wAb[s])
            for kc in range(KC):
                p.op("pe", _mm(psP[:, 0:tn], wP[s][:, kc, :], xnT[:, kc, t0:t0 + tn], kc == 0, kc == KC - 1), [wPb[s]] + xr, [pPb], lhs=wPb[s])
            p.op("dve", _tt(t1[b2][:, 0:tn], psA[:, 0:tn], cosT[:, t0:t0 + tn], ALU.mult), [pAb, tabb], [t1b[b2]])
            p.op("dve", _tt(t2[b2][:, 0:tn], psP[:, 0:tn], sinT[:, t0:t0 + tn], ALU.mult), [pPb, tabb], [t2b[b2]])
            p.op("dve", _tt(dst[:, t0:t0 + tn], t1[b2][:, 0:tn], t2[b2][:, 0:tn], ALU.add), [t1b[b2], t2b[b2]], [qkb[c]])
    p.barrier()
    ar.pop()
    vp_flat = ar.alloc(NT * 2 * 4 * 128, BF16)
    vp = vp_flat.rearrange("p (t h g c) -> p t h g c", t=NT, h=2, g=4)
    vpb = [Buf(f"vp{t}") for t in range(NT)]
    p.op("pool", _memset(vp_flat, 0.0), [], vpb)
    ar.push()
    wv = ar.alloc(KC * 256, BF16).rearrange("p (a b) -> p a b", a=KC)
    wvb = Buf("wv")
    p.op("pool", _dma(wv, wq_v[:, :, 2304:2560]), [], [wvb], dsem="wv")
    for t in range(NT):
        b2 = t % 2
        psV, pVb = k.psum[4 + b2], k.psb[4 + b2]
        for kc in range(KC):
            p.op("pe", _mm(psV[:, 0:256], xnT[:, kc, t * 128:(t + 1) * 128], wv[:, kc, :], kc == 0, kc == KC - 1),
                 xnTb[t] + [wvb], [pVb], lhs=xnTb[t][kc // 4])
        src = psV[:, 0:256].rearrange("p (g c) -> p g c", g=4)
        p.op("act", lambda e, o=vp[:, t, 0, :, 0:64], i=src: e.copy(out=o, in_=i), [pVb], [vpb[t]])
        p.op("dve", _copy(vp[:, t, 1, :, 64:128], src), [pVb], [vpb[t]])
    p.barrier()
    ar.pop()
    OT = xnT
    OTb = [[Buf(f"OT{c}_{t}") for t in range(NT)] for c in range(16)]
    mask3 = ar.alloc(384, BF16)
    mask0 = ar.alloc(128, BF16)
    ones_p = ar.alloc(2 * 128, BF16).rearrange("p (h c) -> p h c", h=2)
    onesm_p = ar.alloc(2 * 128, BF16).rearrange("p (h c) -> p h c", h=2)
    es = ar.alloc(16, F32)
    cb = Buf("swaconst")
    p.op("pool", _dma(mask3, k.c_mask3[:, :]), [], [cb], dsem="const")
    p.op("pool", _dma(mask0, k.c_mask0[:, :]), [], [cb], dsem="const")
    p.op("pool", _dma(ones_p.rearrange("p h c -> p (h c)"), k.c_onesp[:, :]), [], [cb], dsem="const")
    p.op("pool", _dma(onesm_p.rearrange("p h c -> p (h c)"), k.c_onesmp[:, :]), [], [cb], dsem="const")
    esb = Buf("es")
    for hf in range(2):
        p.op("sp", _dma(es[hf * 64:(hf + 1) * 64, :], sinks2[hf, :].partition_broadcast(64)), [], [esb], dsem="es")
    p.op("act", _act(es, es, AF.Exp), [esb], [esb])
    ex = [[ar.alloc(384, BF16) for _ in range(2)] for _ in range(2)]
    exb = [[Buf(), Buf()], [Buf(), Buf()]]
    PT = [[ar.alloc(384, BF16) for _ in range(2)] for _ in range(2)]
    PTb = [[Buf(), Buf()], [Buf(), Buf()]]
    dn = [ar.alloc(128, F32) for _ in range(2)]
    dnb = [Buf(), Buf()]
    it = 0
    for c in range(16):
        g = (2 * c) // 8
        for qt in range(NT):
            b2 = it % 2
            it += 1
            if qt == 0:
                kbs = [(0, "m0")]
            elif qt == 1:
                kbs = [(1, "cur"), (0, "meta")]
            else:
                kbs = [(qt, "cur"), (0, "meta"), (qt - 1, "prev")]
            nk = len(kbs)
            for hf in range(2):
                psS, pSb = k.psum[b2 * 2 + hf], k.psb[b2 * 2 + hf]
                rows = slice(hf * 64, (hf + 1) * 64)
                for bi, (kt, kind) in enumerate(kbs):
                    p.op("pe", _mm(psS[:, bi * 128:(bi + 1) * 128], kT[rows, g, kt * 128:(kt + 1) * 128], qT[rows, c, qt * 128:(qt + 1) * 128], True, True),
                         [qkb[16 + g], qkb[c]], [pSb], lhs=qkb[16 + g])
                p.op("act", _act(ex[b2][hf][:, 0:nk * 128], psS[:, 0:nk * 128], AF.Exp, scale=0.125), [pSb], [exb[b2][hf]])
                mk = mask0 if qt == 0 else mask3[:, 0:nk * 128]
                p.op("dve", _tt(PT[b2][hf][:, 0:nk * 128], ex[b2][hf][:, 0:nk * 128], mk, ALU.mult), [exb[b2][hf], cb], [PTb[b2][hf]])
            psO, pOb = k.psum[4 + b2], k.psb[4 + b2]
            psDn, pDb = k.psum[6 + b2], k.psb[6 + b2]
            nmm = 2 * nk
            i = 0
            for hf in range(2):
                for bi, (kt, kind) in enumerate(kbs):
                    p.op("pe", _mm(psO[:, 0:128], vp[:, kt, hf, g, :], PT[b2][hf][:, bi * 128:(bi + 1) * 128], i == 0, i == nmm - 1),
                         [vpb[kt], PTb[b2][hf]], [pOb], lhs=vpb[kt])
                    i += 1
            i = 0
            for hf in range(2):
                for bi, (kt, kind) in enumerate(kbs):
                    on = onesm_p if kind in ("meta", "m0") else ones_p
                    p.op("pe", _mm(psDn[:, 0:128], on[:, hf, :], PT[b2][hf][:, bi * 128:(bi + 1) * 128], i == 0, i == nmm - 1),
                         [cb, PTb[b2][hf]], [pDb], lhs=cb)
                    i += 1
            p.op("act", _act(dn[b2], psDn[:, 0:128], AF.Ln, bias=es[:, c:c + 1]), [pDb, esb], [dnb[b2]])
            p.op("act", _act(dn[b2], dn[b2], AF.Exp, scale=-1.0), [dnb[b2]], [dnb[b2]])
            p.op("dve", _tt(OT[:, c, qt * 128:(qt + 1) * 128], psO[:, 0:128], dn[b2], ALU.mult), [pOb, dnb[b2]], [OTb[c][qt]])
    p.barrier()
    ar.pop()
    ar.push()
    OT2 = ar.alloc(KC * NTP, BF16).rearrange("p (a b) -> p a b", a=KC)
    out_proj(k, OT2, [OTb[c] for c in range(16)], 16, w_out, 1.0)
    ar.pop()


def _bc(ap2, n=128):
    return ap2.unsqueeze(2).broadcast_to([128, ap2.shape[1], n])


def _bo(ap2, m):
    return ap2.unsqueeze(1).broadcast_to([128, m, ap2.shape[1]])


def _v4(ap):
    return ap.rearrange("p (h c) -> p h c", c=128)


def _rep2(ap256):
    return ap256.rearrange("p (q c) -> p q c", q=2).unsqueeze(2).broadcast_to([128, 2, 2, 128])


def _bc22(ap4):
    return ap4.rearrange("p (q r) -> p q r", q=2).unsqueeze(3).broadcast_to([128, 2, 2, 128])


def _v22(ap512):
    return ap512.rearrange("p (q r c) -> p q r c", q=2, r=2)


def dn_stage(k, i, j):
    ins = k.ins
    p, ar, nc = k.p, k.ar, k.nc
    gain_row = ins["mix_norm"][i, :]
    w_in = ins["dn_w_in"][j * D:(j + 1) * D, :].rearrange("(kc p) n -> p kc n", p=128)
    w_out = ins["dn_w_out"][j * 4096:(j + 1) * 4096, :]
    sfx = f"_{i}"
    dk_ = "ExternalOutput" if getattr(k, "debug", False) else "Internal"
    pfx = "dbg_" if getattr(k, "debug", False) else ""
    phases = getattr(k, "dn_phases", "PRO")
    qk_s = nc.dram_tensor(pfx + "qk_s" + sfx, [32, 128, NTP], BF16, kind=dk_).ap()
    v_s = nc.dram_tensor(pfx + "v_s" + sfx, [32, 128, NTP], BF16, kind=dk_).ap()
    z_s = nc.dram_tensor(pfx + "z_s" + sfx, [NTP, 4096], BF16, kind=dk_).ap()
    oT_s = nc.dram_tensor(pfx + "oT_s" + sfx, [32, 128, NTP], BF16, kind=dk_).ap()
    qksb = [Buf(f"qks{c}") for c in range(32)]
    vsb = [Buf(f"vs{c}") for c in range(32)]
    zsb = [Buf(f"zs{t}") for t in range(NT)]
    oTb = [[Buf(f"oTs{hv}_{t}") for t in range(NT)] for hv in range(32)]

    ar.push()
    g_all = ar.alloc(NT * 32, F32).rearrange("p (t h) -> p t h", t=NT)
    beta_all = ar.alloc(NT * 32, F32).rearrange("p (t h) -> p t h", t=NT)
    gab = [Buf(f"g{t}") for t in range(NT)]
    bab = [Buf(f"b{t}") for t in range(NT)]
    ones_f = ar.alloc(128, F32)
    onesb = Buf("ones")
    p.op("pool", _memset(ones_f, 1.0), [], [onesb])
    ar.push()
    xnT = ar.alloc(KC * NTP, BF16).rearrange("p (a b) -> p a b", a=KC)
    xnTb = [[Buf(f"xnT{t}_{q}") for q in range(4)] for t in range(NT)]
    norm_pass(k, gain_row, xnT, xnTb)
    xall = [b for t in range(NT) for b in xnTb[t]]
    cw = ar.alloc(64 * 4, F32).rearrange("p (c k) -> p c k", k=4)
    cwb = Buf("cw")
    p.op("sp", _dma(cw.rearrange("p c k -> p (c k)"), ins["dn_conv_w"][j * 128:(j + 1) * 128, :]), [], [cwb], dsem="cw")
    dtb = ar.alloc(32, F32)
    negA = ar.alloc(32, F32)
    smb = Buf("dnsmall")
    p.op("sp", _dma(dtb, ins["dn_dt_bias"][j, :].partition_broadcast(128)), [], [smb], dsem="cw")
    p.op("sp", _dma(negA, ins["dn_a_log"][j, :].partition_broadcast(128)), [], [smb], dsem="cw")
    p.op("act", _act(negA, negA, AF.Exp), [smb], [smb])
    p.op("dve", _ts(negA, negA, -1.0, None, ALU.mult), [smb], [smb])
    ws = [ar.alloc(KC * 256, BF16).rearrange("p (a b) -> p a b", a=KC) for _ in range(2)]
    wsb = [Buf("ws0"), Buf("ws1")]
    cbuf = [ar.alloc(NTP + 4, F32) for _ in range(2)]
    cbb = [Buf("cb0"), Buf("cb1")]
    for s_ in range(2):
        p.op("pool", _memset(cbuf[s_][:, 0:4], 0.0), [], [cbb[s_]])
    acc = [ar.alloc(NTP, F32) for _ in range(2)]
    accb = [[Buf()], [Buf()]]
    yb2 = [ar.alloc(NTP, F32) for _ in range(2)]
    ybb2 = [Buf("y0"), Buf("y1")]
    ysq2 = [ar.alloc(NTP, F32) for _ in range(2)]
    ysqb2 = [Buf("ysq0"), Buf("ysq1")]
    pending = []
    ybf = [ar.alloc(NTP, BF16) for _ in range(2)]
    ybfb = [Buf("ybf0"), Buf("ybf1")]
    tmpn = [ar.alloc(512, F32) for _ in range(2)]
    tmpnb = [Buf(), Buf()]
    CS = 1280
    bank = [0]

    def nb():
        b = bank[0]
        bank[0] = (b + 1) % 8
        return b

    for ch in range(64):
        s = (ch // 2) % 2
        if ch % 2 == 0:
            p.op("pool", _dma(ws[s], w_in[:, :, ch * 128:(ch + 2) * 128]), [], [wsb[s]], dsem=f"ws{s}")
        q = ch % 2
        cs = ch % 2
        for (t0, tn) in TG:
            b = nb()
            xr = [bb for t in range(t0 // 128, (t0 + tn) // 128) for bb in xnTb[t]]
            for kc in range(KC):
                p.op("pe", _mm(k.psum[b][:, 0:tn], ws[s][:, kc, q * 128:(q + 1) * 128], xnT[:, kc, t0:t0 + tn], kc == 0, kc == KC - 1),
                     [wsb[s]] + xr, [k.psb[b]], lhs=wsb[s])
            p.op("act", lambda e, o=cbuf[cs][:, 3 + t0:3 + t0 + tn], i_=k.psum[b][:, 0:tn]: e.copy(out=o, in_=i_), [k.psb[b]], [cbb[cs]])
        for hi, (eng, c0, c1) in enumerate((("dve", 0, NTP),)):
            for kk in range(4):
                src = cbuf[cs][:, c0 + kk:c1 + kk]
                if kk == 0:
                    p.op(eng, _ts(acc[cs][:, c0:c1], src, cw[:, ch, 0:1], None, ALU.mult), [cbb[cs], cwb], [accb[cs][hi]])
                else:
                    p.op(eng, _stt(acc[cs][:, c0:c1], src, cw[:, ch, kk:kk + 1], acc[cs][:, c0:c1], ALU.mult, ALU.add),
                         [cbb[cs], cwb, accb[cs][hi]], [accb[cs][hi]])
        if ch >= 32:
            p.op("act", _act(ybf[cs], acc[cs], AF.Silu), accb[cs], [ybfb[cs]])
            p.op("sp", _dma(v_s[ch - 32], ybf[cs]), [ybfb[cs]], [vsb[ch - 32]], dsem=f"ybf{cs}")
        else:
            yb, ybb, ysq, ysqb = yb2[cs], ybb2[cs], ysq2[cs], ysqb2[cs]
            p.op("act", _act(yb, acc[cs], AF.Silu), accb[cs], [ybb])
            p.op("act", _act(ysq, yb, AF.Square), [ybb], [ysqb])

            def l2tail(ch=ch, cs=cs, yb=yb, ybb=ybb, ysq=ysq, ysqb=ysqb):
                sc = (128.0 ** -0.5) if ch < 16 else 1.0
                for ti, (t0, tn) in enumerate(TG):
                    b = nb()
                    p.op("pe", _mm(k.psum[b][:, 0:tn], ones_f, ysq[:, t0:t0 + tn], True, True), [onesb, ysqb], [k.psb[b]], lhs=onesb)
                    p.op("act", _act(tmpn[ti % 2][:, 0:tn], k.psum[b][:, 0:tn], AF.Ln, bias=k.eps_ap), [k.psb[b], k.epsb], [tmpnb[ti % 2]])
                    p.op("act", _act(tmpn[ti % 2][:, 0:tn], tmpn[ti % 2][:, 0:tn], AF.Exp, scale=-0.5), [tmpnb[ti % 2]], [tmpnb[ti % 2]])
                    p.op("dve", _stt(ybf[cs][:, t0:t0 + tn], yb[:, t0:t0 + tn], sc, tmpn[ti % 2][:, 0:tn], ALU.mult, ALU.mult),
                         [ybb, tmpnb[ti % 2]], [ybfb[cs]])
                p.op("sp", _dma(qk_s[ch], ybf[cs]), [ybfb[cs]], [qksb[ch]], dsem=f"ybf{cs}")
            pending.append(l2tail)
        while len(pending) > (1 if ch < 31 else 0):
            pending.pop(0)()
    wz = [ar.alloc(KC * 512, BF16).rearrange("p (a b) -> p a b", a=KC) for _ in range(2)]
    wzb = [Buf("wz0"), Buf("wz1")]
    zst = [ar.alloc(512, BF16) for _ in range(2)]
    zstb = [Buf(), Buf()]
    zi = 0
    for cg in range(8):
        s = cg % 2
        p.op("pool", _dma(wz[s], w_in[:, :, 8192 + cg * 512:8192 + (cg + 1) * 512]), [], [wzb[s]], dsem=f"wz{s}")
        for t in range(NT):
            b = nb()
            for kc in range(KC):
                p.op("pe", _mm(k.psum[b][:, :], xnT[:, kc, t * 128:(t + 1) * 128], wz[s][:, kc, :], kc == 0, kc == KC - 1),
                     xnTb[t] + [wzb[s]], [k.psb[b]], lhs=xnTb[t][kc // 4])
            zz = zi % 2
            zi += 1
            p.op("act", _act(zst[zz], k.psum[b][:, :], AF.Silu), [k.psb[b]], [zstb[zz]])
            p.op("sp", _dma(z_s[t * 128:(t + 1) * 128, cg * 512:(cg + 1) * 512], zst[zz]), [zstb[zz]], [zsb[t]], dsem=f"zst{zz}")
    wba = ar.alloc(KC * 64, BF16).rearrange("p (a b) -> p a b", a=KC)
    wbab = Buf("wba")
    p.op("pool", _dma(wba, w_in[:, :, 12288:12352]), [], [wbab], dsem="wba")
    ta = [ar.alloc(32, F32) for _ in range(2)]
    tab_ = [Buf(), Buf()]
    for t in range(NT):
        b = nb()
        for kc in range(KC):
            p.op("pe", _mm(k.psum[b][:, 0:64], xnT[:, kc, t * 128:(t + 1) * 128], wba[:, kc, :], kc == 0, kc == KC - 1),
                 xnTb[t] + [wbab], [k.psb[b]], lhs=xnTb[t][kc // 4])
        p.op("act", _act(beta_all[:, t, :], k.psum[b][:, 0:32], AF.Sigmoid), [k.psb[b]], [bab[t]])
        p.op("dve", _tt(ta[t % 2], k.psum[b][:, 32:64], dtb, ALU.add), [k.psb[b], smb], [tab_[t % 2]])
        p.op("act", _act(ta[t % 2], ta[t % 2], AF.Exp), [tab_[t % 2]], [tab_[t % 2]])
        p.op("act", _act(ta[t % 2], ta[t % 2], AF.Ln, bias=1.0), [tab_[t % 2]], [tab_[t % 2]])
        p.op("dve", _tt(g_all[:, t, :], ta[t % 2], negA, ALU.mult), [tab_[t % 2], smb], [gab[t]])
    dump(k, "g_all" + sfx, g_all.rearrange("p t h -> p (t h)"), gab)
    dump(k, "beta_all" + sfx, beta_all.rearrange("p t h -> p (t h)"), bab)
    p.barrier()
    ar.pop()
    if "R" not in phases:
        ar.pop()
        return
    ar.push()
    ident_f = ar.alloc(128, F32)
    ui_f = ar.alloc(128, F32)
    sl_f = ar.alloc(128, F32)
    og_bc = ar.alloc(128, F32)
    cfb = Buf("dnconst")
    p.op("sp", _dma(ident_f, k.ident_dram[:, :]), [], [cfb], dsem="cw")
    p.op("sp", _dma(ui_f, k.c_ui[:, :]), [], [cfb], dsem="cw")
    p.op("sp", _dma(sl_f, k.c_sl[:, :]), [], [cfb], dsem="cw")
    p.op("sp", _dma(og_bc, ins["dn_out_norm"][j, :].partition_broadcast(128)), [], [cfb], dsem="cw")
    S_f = ar.alloc(32 * 128, F32)
    S_b = ar.alloc(32 * 128, BF16)
    Sfb = [Buf(f"Sf{g}") for g in range(8)]
    Sbb = [Buf(f"Sb{g}") for g in range(8)]
    p.op("pool", _memset(S_f, 0.0), [], Sfb)
    p.op("pool", _memset(S_b, 0.0), [], Sbb)
    qkT = [ar.alloc(32 * 128, BF16).rearrange("p (h c) -> p h c", h=32) for _ in range(2)]
    vT = [ar.alloc(32 * 128, BF16).rearrange("p (h c) -> p h c", h=32) for _ in range(2)]
    zt = [ar.alloc(4096, BF16) for _ in range(2)]
    qkTb = [Buf("qkT0"), Buf("qkT1")]
    vTb = [Buf("vT0"), Buf("vT1")]
    ztb = [Buf("zt0"), Buf("zt1")]
    NSC = 7
    gcgl = [ar.alloc(64, F32) for _ in range(2)]
    sc_t = [[gcgl[i_][:, 0:32]] + [ar.alloc(32, F32) for _ in range(NSC - 1)] for i_ in range(2)]
    sc_b = [[Buf() for _ in range(NSC)] for _ in range(2)]

    def f32x4():
        return ar.alloc(512, F32)

    def bf16x4():
        return ar.alloc(512, BF16)

    class PB:
        pass
    pbs = []
    for par in range(2):
        o = PB()
        for nm in ("Gd", "d4", "E4", "tmp4", "tmpa", "L0", "L1", "M0", "M1", "P0", "P1", "u4"):
            setattr(o, nm, f32x4())
            setattr(o, nm + "_b", Buf(nm))
        for nm in ("attnT", "kgl", "wT", "vb", "kbg", "P6b"):
            setattr(o, nm, bf16x4())
            setattr(o, nm + "_b", Buf(nm))
        pbs.append(o)
    vnew = [bf16x4() for _ in range(2)]
    vnewb = [Buf(), Buf()]
    tq = [f32x4() for _ in range(2)]
    tqb = [Buf(), Buf()]
    o4 = [f32x4() for _ in range(2)]
    o4b = [Buf(), Buf()]
    osq = f32x4()
    osqb = Buf()
    ss4 = [ar.alloc(4, F32) for _ in range(2)]
    ss4b = [Buf(), Buf()]
    og4 = [f32x4() for _ in range(2)]
    og4b = [Buf(), Buf()]
    ogb4 = [bf16x4() for _ in range(2)]
    ogb4b = [Buf(), Buf()]
    ogT = [bf16x4() for _ in range(2)]
    ogTb = [Buf(), Buf()]
    ident_b = k.ident

    def prep(t, grp, par, ts):
        B = pbs[par]
        s = t % 2
        hv0, hq0 = 4 * grp, 2 * grp
        gc, eg, egla, egl, beg, negbet, dgl = sc_t[ts]
        gcb, egb, eglab, eglb, begb, negbetb, dglb = sc_b[ts]
        gc4, bet4 = gc[:, hv0:hv0 + 4], beta_all[:, t, hv0:hv0 + 4]
        bX, bY, bZ = 3 * par, 3 * par + 1, 3 * par + 2
        psX, psY, psZ = k.psum[bX], k.psum[bY], k.psum[bZ]
        for r in range(2):
            p.op("pe", _mm(psZ[:, r * 128:(r + 1) * 128], qkT[s][:, 16 + hq0 + r, :], ident_b, True, True), [qkTb[s], k.identb], [k.psb[bZ]], lhs=qkTb[s])
        for r in range(4):
            p.op("pe", _mm(psY[:, r * 128:(r + 1) * 128], vT[s][:, hv0 + r, :], ident_b, True, True), [vTb[s], k.identb], [k.psb[bY]], lhs=vTb[s])
        p.op("dve", _tt(_v4(B.Gd), _bo(ident_f, 4), _bc(gc4), ALU.mult), [cfb, gcb], [B.Gd_b])
        p.op("pe", _mm(psX[:, :], ones_f, B.Gd, True, True), [onesb, B.Gd_b], [k.psb[bX]], lhs=B.Gd_b)
        yield
        p.op("dve", _tt(_v4(B.d4), _v4(psX), _bc(gc4), ALU.subtract), [k.psb[bX], gcb], [B.d4_b])
        p.op("dve", _stt(B.d4, B.d4, -1.0, B.d4, ALU.mult, ALU.min), [B.d4_b], [B.d4_b])
        p.op("act", _act(B.E4, B.d4, AF.Exp), [B.d4_b], [B.E4_b])
        for r in range(2):
            p.op("pe", _mm(psX[:, r * 128:(r + 1) * 128], qkT[s][:, 16 + hq0 + r, :], qkT[s][:, 16 + hq0 + r, :], True, True), [qkTb[s]], [k.psb[bX]], lhs=qkTb[s])
        for r in range(2):
            p.op("pe", _mm(psX[:, 256 + r * 128:256 + (r + 1) * 128], qkT[s][:, 16 + hq0 + r, :], qkT[s][:, hq0 + r, :], True, True), [qkTb[s]], [k.psb[bX]], lhs=qkTb[s])
        p.op("dve", _tt(_v4(B.vb), _v4(psY), _bc(bet4), ALU.mult), [k.psb[bY], bab[t]], [B.vb_b])
        p.op("dve", _tt(_v22(B.kbg), _rep2(psZ[:, 0:256]), _bc22(beg[:, hv0:hv0 + 4]), ALU.mult), [k.psb[bZ], begb], [B.kbg_b])
        p.op("dve", _tt(_v22(B.kgl), _rep2(psZ[:, 0:256]), _bc22(egl[:, hv0:hv0 + 4]), ALU.mult), [k.psb[bZ], eglb], [B.kgl_b])
        yield
        p.op("dve", _tt(_v22(B.tmp4), _rep2(psX[:, 0:256]), _v22(B.E4), ALU.mult), [k.psb[bX], B.E4_b], [B.tmp4_b])
        p.op("pool", _tt(_v4(B.tmp4), _v4(B.tmp4), _bc(negbet[:, hv0:hv0 + 4]), ALU.mult), [B.tmp4_b, negbetb], [B.tmp4_b])
        p.op("pool", _tt(_v4(B.L0), _v4(B.tmp4), _bo(sl_f, 4), ALU.mult), [B.tmp4_b, cfb], [B.L0_b])
        p.op("dve", _tt(_v22(B.tmpa), _rep2(psX[:, 256:512]), _v22(B.E4), ALU.mult), [k.psb[bX], B.E4_b], [B.tmpa_b])
        p.op("pool", _tt(_v4(B.attnT), _v4(B.tmpa), _bo(ui_f, 4), ALU.mult), [B.tmpa_b, cfb], [B.attnT_b])
        yield
        for r in range(4):
            p.op("pe", _mm(psY[:, r * 128:(r + 1) * 128], B.L0[:, r * 128:(r + 1) * 128], ident_f, True, True), [B.L0_b, cfb], [k.psb[bY]], lhs=B.L0_b)
        p.op("act", lambda e, o=B.M0, i_=psY: e.copy(out=o, in_=i_), [k.psb[bY]], [B.M0_b])
        p.op("pool", _tt(_v4(B.P0), _v4(B.M0), _bo(ident_f, 4), ALU.add), [B.M0_b, cfb], [B.P0_b])
        yield
        L = [(B.L0, B.L0_b), (B.L1, B.L1_b)]
        M = [(B.M0, B.M0_b), (B.M1, B.M1_b)]
        P = [(B.P0, B.P0_b), (B.P1, B.P1_b)]
        for m in range(6):
            (Lc, Lcb), (Ln, Lnb) = L[m % 2], L[(m + 1) % 2]
            (Mc, Mcb), (Mn, Mnb) = M[m % 2], M[(m + 1) % 2]
            (Pc, Pcb), (Pn, Pnb) = P[m % 2], P[(m + 1) % 2]
            for r in range(4):
                cs_ = slice(r * 128, (r + 1) * 128)
                p.op("pe", _mm(psX[:, cs_], Mc[:, cs_], Lc[:, cs_], True, True), [Mcb, Lcb], [k.psb[bX]], lhs=Mcb)
            if m < 5:
                for r in range(4):
                    cs_ = slice(r * 128, (r + 1) * 128)
                    p.op("pe", _mm(psY[:, cs_], Lc[:, cs_], Mc[:, cs_], True, True), [Mcb, Lcb], [k.psb[bY]], lhs=Lcb)
            p.op("dve", _copy(Ln, psX), [k.psb[bX]], [Lnb])
            if m < 5:
                p.op("act", lambda e, o=Mn, i_=psY: e.copy(out=o, in_=i_), [k.psb[bY]], [Mnb])
            for r in range(4):
                cs_ = slice(r * 128, (r + 1) * 128)
                p.op("pe", _mm(psZ[:, cs_], Ln[:, cs_], Pc[:, cs_], True, True), [Lnb, Pcb], [k.psb[bZ]], lhs=Lnb)
            if m == 5:
                p.op("dve", _tt(B.P6b, psZ, Pc, ALU.add), [k.psb[bZ], Pcb], [B.P6b_b])
            else:
                p.op("dve", _tt(Pn, psZ, Pc, ALU.add), [k.psb[bZ], Pcb], [Pnb])
            yield
        Pf, Pfb = B.P6b, B.P6b_b
        for r in range(4):
            cs_ = slice(r * 128, (r + 1) * 128)
            p.op("pe", _mm(psX[:, cs_], Pf[:, cs_], B.vb[:, cs_], True, True), [Pfb, B.vb_b], [k.psb[bX]], lhs=Pfb)
        for r in range(4):
            cs_ = slice(r * 128, (r + 1) * 128)
            p.op("pe", _mm(psY[:, cs_], B.kbg[:, cs_], Pf[:, cs_], True, True), [Pfb, B.kbg_b], [k.psb[bY]], lhs=B.kbg_b)
        p.op("act", lambda e, o=B.u4, i_=psX: e.copy(out=o, in_=i_), [k.psb[bX]], [B.u4_b])
        p.op("dve", _copy(B.wT, psY), [k.psb[bY]], [B.wT_b])
        yield

    seqn = [0]

    def seq(t, grp, par, ts):
        B = pbs[par]
        s = t % 2
        q2 = seqn[0] % 2
        seqn[0] += 1
        hv0, hq0 = 4 * grp, 2 * grp
        gc, eg, egla, egl, beg, negbet, dgl = sc_t[ts]
        gcb, egb, eglab, eglb, begb, negbetb, dglb = sc_b[ts]
        S4f = S_f[:, hv0 * 128:(hv0 + 4) * 128]
        S4b = S_b[:, hv0 * 128:(hv0 + 4) * 128]
        psW, psQ, psO, psS = k.psum[6], k.psum[7], k.psum[6], k.psum[7]
        bW, bQ, bO, bS = 6, 7, 6, 7
        for r in range(4):
            cs_ = slice(r * 128, (r + 1) * 128)
            p.op("pe", _mm(psW[:, cs_], B.wT[:, cs_], S4b[:, cs_], True, True), [B.wT_b, Sbb[grp]], [k.psb[bW]], lhs=B.wT_b)
        for r in range(4):
            cs_ = slice(r * 128, (r + 1) * 128)
            p.op("pe", _mm(psQ[:, cs_], qkT[s][:, hq0 + r // 2, :], S4b[:, cs_], True, True), [qkTb[s], Sbb[grp]], [k.psb[bQ]], lhs=qkTb[s])
        p.op("dve", _tt(vnew[q2], B.u4, psW, ALU.subtract), [B.u4_b, k.psb[bW]], [vnewb[q2]])
        p.op("dve", _tt(_v4(tq[q2]), _v4(psQ), _bc(eg[:, hv0:hv0 + 4]), ALU.mult), [k.psb[bQ], egb], [tqb[q2]])
        for r in range(4):
            cs_ = slice(r * 128, (r + 1) * 128)
            p.op("pe", _mm(psO[:, cs_], B.attnT[:, cs_], vnew[q2][:, cs_], True, True), [B.attnT_b, vnewb[q2]], [k.psb[bO]], lhs=B.attnT_b)
        for r in range(4):
            cs_ = slice(r * 128, (r + 1) * 128)
            p.op("pe", _mm(psS[:, cs_], B.kgl[:, cs_], vnew[q2][:, cs_], True, True), [B.kgl_b, vnewb[q2]], [k.psb[bS]], lhs=B.kgl_b)
        p.op("dve", _tt(o4[q2], tq[q2], psO, ALU.add), [tqb[q2], k.psb[bO]], [o4b[q2]])
        p.op("pool", _tt(_v4(S4f), _v4(S4f), _bc(egla[:, hv0:hv0 + 4]), ALU.mult), [Sfb[grp], eglab], [Sfb[grp]])
        p.op("dve", _tt(S4f, S4f, psS, ALU.add), [Sfb[grp], k.psb[bS]], [Sfb[grp]])
        p.op("act", lambda e, o=S4b, i_=S4f: e.copy(out=o, in_=i_), [Sfb[grp]], [Sbb[grp]])
        p.op("pool", _tt(osq, o4[q2], o4[q2], ALU.mult), [o4b[q2]], [osqb])
        p.op("dve", lambda e, o=ss4[q2], i_=_v4(osq): e.tensor_reduce(out=o, in_=i_, axis=mybir.AxisListType.X, op=ALU.add), [osqb], [ss4b[q2]])
        p.op("act", _act(ss4[q2], ss4[q2], AF.Sqrt, scale=1.0 / 128.0, bias=k.eps_ap), [ss4b[q2], k.epsb], [ss4b[q2]])
        p.op("dve", lambda e, o=ss4[q2]: e.reciprocal(out=o, in_=o), [ss4b[q2]], [ss4b[q2]])
        p.op("dve", _tt(_v4(og4[q2]), _v4(o4[q2]), _bc(ss4[q2]), ALU.mult), [o4b[q2], ss4b[q2]], [og4b[q2]])
        p.op("pool", _tt(_v4(og4[q2]), _v4(og4[q2]), _bo(og_bc, 4), ALU.mult), [og4b[q2], cfb], [og4b[q2]])
        p.op("pool", _tt(ogb4[q2], og4[q2], zt[s][:, hv0 * 128:(hv0 + 4) * 128], ALU.mult), [og4b[q2], ztb[s]], [ogb4b[q2]])
        for r in range(4):
            cs_ = slice(r * 128, (r + 1) * 128)
            p.op("pe", _mm(psW[:, cs_], ogb4[q2][:, cs_], ident_b, True, True), [ogb4b[q2], k.identb], [k.psb[bW]], lhs=ogb4b[q2])
        p.op("act", lambda e, o=ogT[q2], i_=psW: e.copy(out=o, in_=i_), [k.psb[bW]], [ogTb[q2]])
        p.op("sp", _dma(oT_s[hv0:hv0 + 4, :, t * 128:(t + 1) * 128].rearrange("h p c -> p h c"), _v4(ogT[q2])),
             [ogTb[q2]], [oTb[hv0 + r][t] for r in range(4)], dsem=f"ogT{q2}")

    gctr = 0
    import os
    ntl = int(os.environ.get("DN_TILES", NT))
    maxstep = int(os.environ.get("DN_MAXSTEP", 99))
    def tile_loads(t_):
        s_ = t_ % 2
        for hh in range(4):
            hs = slice(hh * 8, (hh + 1) * 8)
            p.op("sp", _dma(qkT[s_][:, hs, :], qk_s[hs, :, t_ * 128:(t_ + 1) * 128].rearrange("h p c -> p h c")), qksb[hs], [qkTb[s_]], dsem=f"qkT{s_}")
            p.op("sp", _dma(vT[s_][:, hs, :], v_s[hs, :, t_ * 128:(t_ + 1) * 128].rearrange("h p c -> p h c")), vsb[hs], [vTb[s_]], dsem=f"vT{s_}")
        p.op("sp", _dma(zt[s_], z_s[t_ * 128:(t_ + 1) * 128, :]), [zsb[t_]], [ztb[s_]], dsem=f"zt{s_}")

    tile_loads(0)
    for t in range(ntl):
        s = t % 2
        ts = t % 2
        gc, eg, egla, egl, beg, negbet, dgl = sc_t[ts]
        gcb, egb, eglab, eglb, begb, negbetb, dglb = sc_b[ts]
        psG = k.psum[7]
        p.op("pe", _mm(psG[:, 0:32], ui_f, g_all[:, t, :], True, True), [cfb, gab[t]], [k.psb[7]], lhs=cfb)
        p.op("pe", _mm(psG[:, 32:64], ones_f, g_all[:, t, :], True, True), [onesb, gab[t]], [k.psb[7]], lhs=onesb)
        p.op("dve", _copy(gcgl[ts], psG[:, 0:64]), [k.psb[7]], [gcb])
        p.op("act", _act(eg, gcgl[ts][:, 0:32], AF.Exp), [gcb], [egb])
        p.op("act", _act(egla, gcgl[ts][:, 32:64], AF.Exp), [gcb], [eglab])
        p.op("dve", _tt(dgl, gcgl[ts][:, 32:64], gcgl[ts][:, 0:32], ALU.subtract), [gcb], [dglb])
        p.op("act", _act(egl, dgl, AF.Exp), [dglb], [eglb])
        p.op("dve", _tt(beg, beta_all[:, t, :], eg, ALU.mult), [bab[t], egb], [begb])
        p.op("dve", _ts(negbet, beta_all[:, t, :], -1.0, None, ALU.mult), [bab[t]], [negbetb])
        if t + 1 < ntl:
            tile_loads(t + 1)
        active = []
        grp = 0
        while grp < 8 or active:
            while grp < 8 and len(active) < 2:
                active.append((prep(t, grp, gctr % 2, ts), grp, gctr % 2, 0))
                gctr += 1
                grp += 1
            nxt = []
            for (gen, gg, par, nst) in active:
                if nst >= maxstep:
                    continue
                try:
                    next(gen)
                    nxt.append((gen, gg, par, nst + 1))
                except StopIteration:
                    if maxstep >= 99:
                        seq(t, gg, par, ts)
            active = nxt
    p.barrier()
    ar.pop()
    if "O" not in phases:
        ar.pop()
        return
    for half in range(2):
        ar.push()
        oT = ar.alloc(16 * NTP, BF16).rearrange("p (a b) -> p a b", a=16)
        oTl = [Buf(f"oTl{c}") for c in range(16)]
        for c in range(16):
            hv = half * 16 + c
            p.op("sp", _dma(oT[:, c, :], oT_s[hv]), oTb[hv], [oTl[c]], dsem=f"oTl{c % 4}")
        out_proj(k, oT, [[oTl[c]] * NT for c in range(16)], 16, w_out[half * 2048:(half + 1) * 2048, :], 1.0)
        ar.pop()
    ar.pop()


def init_stage(k, x, meta):
    p, ar = k.p, k.ar
    ar.push()
    z = ar.alloc(D, F32)
    zb = Buf("z")
    p.op("dve", _memset(z, 0.0), [], [zb])
    p.op("sp", _dma(k.h[0:112, :], z[0:112, :]), [zb], [k.hb[0]], dsem="h0")
    p.op("sp", _dma(k.h[112:128, :], meta[:, :]), [], [k.hb[0]], dsem="h0")
    for t in range(1, NT):
        p.op("sp", _dma(k.h[t * 128:(t + 1) * 128, :], x[(t - 1) * 128:t * 128, :]), [], [k.hb[t]], dsem=f"h{t}")
    p.op("dve", _memset(k.eps_ap, EPS), [], [k.epsb])
    p.op("pool", _dma(k.ident, k.ident_dram[:, :]), [], [k.identb], dsem="const")
    p.barrier()
    ar.pop()


def final_stage(k, gain_row, out):
    p, ar = k.p, k.ar
    ar.push()
    gain_bc = ar.alloc(D, F32)
    gb = Buf("gain")
    p.op("sp", _dma(gain_bc, gain_row.partition_broadcast(128)), [], [gb], dsem="gain")
    hin = [ar.alloc(D, F32) for _ in range(2)]
    hinb = [Buf(), Buf()]
    ho = [ar.alloc(D, F32) for _ in range(2)]
    hob = [Buf(), Buf()]
    junk = ar.alloc(D, BF16)
    junkb = Buf()
    ss = [ar.alloc(1, F32) for _ in range(2)]
    ssb = [Buf(), Buf()]
    rs = [ar.alloc(1, F32) for _ in range(2)]
    rsb = [Buf(), Buf()]
    ob = Buf("out")
    for t in range(1, NT):
        s = t % 2
        p.op("sp", _dma(hin[s], k.h[t * 128:(t + 1) * 128, :]), [k.hb[t]], [hinb[s]], dsem=f"hin{s}")
        p.op("act", _act(junk, hin[s], AF.Square, accum_out=ss[s]), [hinb[s]], [junkb, ssb[s]])
        p.op("act", _act(ss[s], ss[s], AF.Sqrt, scale=1.0 / D, bias=k.eps_ap), [ssb[s], k.epsb], [ssb[s]])
        p.op("dve", lambda e, o=rs[s], i=ss[s]: e.reciprocal(out=o, in_=i), [ssb[s]], [rsb[s]])
        p.op("dve", _stt(ho[s], hin[s], rs[s][:, 0:1], gain_bc, ALU.mult, ALU.mult), [hinb[s], rsb[s], gb], [hob[s]])
        p.op("sp", _dma(out[(t - 1) * 128:t * 128, :], ho[s]), [hob[s]], [ob], dsem=f"out{s}")
    p.barrier()
    ar.pop()


INPUT_NAMES = ["x", "meta_tokens", "ffn_pre_norm", "ffn_pre_w_gu", "ffn_pre_w_down", "mix_norm",
               "ffn_post_norm", "ffn_post_w_gu", "ffn_post_w_down", "dn_w_in", "dn_conv_w", "dn_a_log",
               "dn_dt_bias", "dn_out_norm", "dn_w_out", "swa_w_qkv", "swa_sinks", "swa_w_out", "final_norm"]

PER_CORE_SHAPES = {
    "x": [SEQ, D], "meta_tokens": [NMETA, D], "ffn_pre_norm": [DEPTH, D], "ffn_pre_w_gu": [DEPTH * D, 2 * DFF],
    "ffn_pre_w_down": [DEPTH * DFF, D], "mix_norm": [DEPTH, D], "ffn_post_norm": [DEPTH, D],
    "ffn_post_w_gu": [DEPTH * D, 2 * DFF], "ffn_post_w_down": [DEPTH * DFF, D],
    "dn_w_in": [2 * D, 12352], "dn_conv_w": [2 * 128, 256], "dn_a_log": [2, 32], "dn_dt_bias": [2, 32],
    "dn_out_norm": [2, 128], "dn_w_out": [2 * 4096, D], "swa_w_qkv": [2 * D, 2560], "swa_sinks": [4, 16],
    "swa_w_out": [2 * D, D], "final_norm": [1, D],
}


def build_program(stages, dbg_h=False, debug=False, only_inputs=None):
    nc = bass.Bass("TRN2", target_bir_lowering=False)
    ins = {n: nc.dram_tensor(n, PER_CORE_SHAPES[n], F32, kind="ExternalInput").ap() for n in INPUT_NAMES
           if only_inputs is None or n in only_inputs}
    ident_dram = nc.dram_tensor("c_ident", [128, 128], F32, kind="ExternalInput").ap()
    cdr = {n: nc.dram_tensor(n, list(a.shape), F32, kind="ExternalInput").ap() for n, a in make_consts().items() if n != "c_ident"}
    out = nc.dram_tensor("out", [SEQ, D], F32, kind="ExternalOutput").ap()
    if dbg_h:
        h = nc.dram_tensor("h_dbg", [NTP, D], F32, kind="ExternalOutput").ap()
    else:
        h = nc.dram_tensor("h_scratch", [NTP, D], F32, kind="Internal").ap()
    import contextlib
    with contextlib.ExitStack() as st:
        big = st.enter_context(nc.sbuf_tensor("big", [128, SB_BYTES // 2], BF16))
        psum = [st.enter_context(nc.psum_tensor(f"ps{i}", [128, 512], F32)) for i in range(8)]
        k = K()
        k.nc = nc
        k.debug = debug
        import os
        k.dn_phases = os.environ.get("DN_PHASES", "PRO")
        k.p = Prog(nc)
        k.ar = Arena(big, SB_BYTES)
        k.psum = [t[:] for t in psum]
        k.psb = [Buf(f"ps{i}") for i in range(8)]
        k.h = h
        k.hb = [Buf(f"h{t}") for t in range(NT)]
        k.ident = k.ar.alloc(128, BF16)
        k.identb = Buf("ident")
        k.ident_dram = ident_dram
        k.c_cos, k.c_sin = cdr["c_cos"], cdr["c_sin"]
        k.c_ui, k.c_sl = cdr["c_ui"], cdr["c_sl"]
        k.c_mask3, k.c_mask0, k.c_onesp, k.c_onesmp = cdr["c_mask3"], cdr["c_mask0"], cdr["c_onesp"], cdr["c_onesmp"]
        k.eps_ap = k.ar.alloc(1, F32)
        k.epsb = Buf("eps")
        k.ins = ins
        for sname in stages:
            if sname == "init":
                init_stage(k, ins["x"], ins["meta_tokens"])
            elif sname.startswith("pre") or sname.startswith("post"):
                which = "pre" if sname.startswith("pre") else "post"
                i = int(sname[len(which):])
                ffn_stage(k, ins[f"ffn_{which}_norm"][i, :],
                          ins[f"ffn_{which}_w_gu"][i * D:(i + 1) * D, :],
                          ins[f"ffn_{which}_w_down"][i * DFF:(i + 1) * DFF, :])
            elif sname.startswith("mix"):
                i = int(sname[3:])
                j = i // 2
                if i % 2 == 1:
                    swa_stage(k, ins["mix_norm"][i, :], ins["swa_w_qkv"][j * D:(j + 1) * D, :],
                              ins["swa_sinks"][2 * j:2 * j + 2, :], ins["swa_w_out"][j * D:(j + 1) * D, :])
                else:
                    dn_stage(k, i, j)
            elif sname == "final":
                final_stage(k, ins["final_norm"][0, :], out)
            else:
                raise ValueError(sname)
        k.p.emit()
    return nc, k.p.nops


def make_consts():
    c = {}
    c["c_ident"] = np.eye(128, dtype=np.float32)
    pidx = np.arange(128)
    col = np.arange(NTP)
    pos = np.maximum(col - 112, 0).astype(np.float32)
    inv = (10000.0 ** (-np.arange(0, 64, 2, dtype=np.float32) / 64.0)).astype(np.float32)
    f = (pidx % 64) % 32
    ang = pos[None, :] * inv[f][:, None]
    sign = np.where((pidx % 64) < 32, -1.0, 1.0).astype(np.float32)
    c["c_cos"] = np.cos(ang).astype(np.float32)
    c["c_sin"] = (np.sin(ang) * sign[:, None]).astype(np.float32)
    jj = np.arange(128)[:, None]
    ii = np.arange(128)[None, :]
    cur = (jj <= ii).astype(np.float32)
    prev = (jj > ii).astype(np.float32)
    c["c_mask3"] = np.concatenate([cur, np.ones((128, 128), np.float32), prev], axis=1)
    c["c_mask0"] = ((jj <= ii) & (jj >= 112)).astype(np.float32)
    op = np.zeros((128, 2, 128), np.float32)
    op[:, 0, 0:64] = 1.0
    op[:, 1, 64:128] = 1.0
    c["c_onesp"] = op.reshape(128, 256)
    om = op.copy()
    om[:112] = 0.0
    c["c_onesmp"] = om.reshape(128, 256)
    pp = np.arange(128)[:, None]
    ff = np.arange(128)[None, :]
    c["c_ui"] = (ff >= pp).astype(np.float32)
    c["c_sl"] = (ff < pp).astype(np.float32)
    return c


def make_in_maps(inputs, n_cores=8, only_inputs=None):
    shared = dict(make_consts())
    for n in INPUT_NAMES:
        if n == "x" or (only_inputs is not None and n not in only_inputs):
            continue
        a = np.ascontiguousarray(np.asarray(inputs[n], dtype=np.float32))
        if n == "dn_conv_w":
            a = np.ascontiguousarray(a.reshape(a.shape[0], 4, 64, 128).transpose(0, 3, 2, 1))
        if n == "swa_sinks":
            a = np.ascontiguousarray(a.reshape(a.shape[0], 16, 2).transpose(0, 2, 1))
        shared[n] = a.reshape(PER_CORE_SHAPES[n])
    x = np.asarray(inputs["x"], dtype=np.float32)
    maps = []
    for c in range(n_cores):
        m = dict(shared)
        m["x"] = np.ascontiguousarray(x[c])
        maps.append(m)
    return maps


ALL_STAGES = ["init"] + [s for i in range(DEPTH) for s in (f"pre{i}", f"mix{i}", f"post{i}")] + ["final"]


def kernel(**inputs):
    nc, _ = build_program(ALL_STAGES)
    maps = make_in_maps(inputs)
    res = run_bass_kernel_spmd(nc, maps, core_ids=list(range(8)))
    return np.stack([r["out"] for r in res.results], axis=0).astype(np.float32)
```
